# Optimizing a Trainium2 kernel written in Bass

```python
import jax, jax.numpy as jnp
from jax import lax
import numpy as np

D_MODEL = 1024
BATCH = 8
SEQ = 2048
DEPTH = 1

D_MIX = 1024
POOL_WINDOWS = (2, 4, 8, 16)
POOL_WIDTH = 512
POOL_GROUP = POOL_WIDTH // len(POOL_WINDOWS)
N_HEADS = 8
N_KV_HEADS = 2
HEAD_DIM = 64
ATTN_WIDTH = N_HEADS * HEAD_DIM
KV_WIDTH = N_KV_HEADS * HEAD_DIM
N_IDX_HEADS = 8
IDX_DIM = 64
TOPK_MAX = 256
ROPE_THETA = 10000.0
Q_BLOCK = 128
EPS = 1e-6
SPLITS = (POOL_WIDTH, POOL_WIDTH, ATTN_WIDTH, KV_WIDTH, KV_WIDTH, ATTN_WIDTH,
          N_IDX_HEADS * IDX_DIM, IDX_DIM, N_IDX_HEADS)
D_IN_PROJ = 2888

kernel_name = "hybrid_pool_dsa_parallel_heads"


def rms_norm(x, w):
    xf = x.astype(jnp.float32)
    y = xf * lax.rsqrt(jnp.mean(xf * xf, axis=-1, keepdims=True) + EPS)
    return (y * w.astype(jnp.float32)).astype(x.dtype)


def rope(x, pos):
    d = x.shape[-1]
    half = d // 2
    freqs = ROPE_THETA ** (-jnp.arange(half, dtype=jnp.float32) / half)
    ang = pos.astype(jnp.float32)[:, None] * freqs[None, :]
    cos = jnp.cos(ang)[None, :, None, :]
    sin = jnp.sin(ang)[None, :, None, :]
    xf = x.astype(jnp.float32)
    x1, x2 = xf[..., :half], xf[..., half:]
    out = jnp.concatenate([x1 * cos - x2 * sin, x2 * cos + x1 * sin], axis=-1)
    return out.astype(x.dtype)


def pool_mixer(u, w_pool, pool_scale):
    B, S, _ = u.shape
    ug = u.reshape(B, S, len(POOL_WINDOWS), POOL_GROUP).astype(jnp.float32)
    cs = jnp.cumsum(ug, axis=1)
    n_avail = jnp.arange(1, S + 1)
    means = []
    for g, w in enumerate(POOL_WINDOWS):
        csg = cs[:, :, g]
        lag = jnp.pad(csg, ((0, 0), (w, 0), (0, 0)))[:, :S]
        cnt = jnp.minimum(n_avail, w).astype(jnp.float32)[None, :, None]
        means.append((csg - lag) / cnt)
    pooled = jnp.stack(means, axis=2) - ug
    mixed = jnp.einsum('bsgc,gcd->bsgd', pooled, w_pool.astype(jnp.float32))
    mixed = mixed.reshape(B, S, POOL_WIDTH) * pool_scale.astype(jnp.float32)
    return mixed.astype(u.dtype)


def dsa_attention(q, k, v, q_idx, k_idx, w_idx):
    B, S = q.shape[0], q.shape[1]
    nb = S // Q_BLOCK
    k_top = min(TOPK_MAX, S // 4)
    grp = N_HEADS // N_KV_HEADS
    key_pos = jnp.arange(S)
    kf_idx = k_idx.astype(jnp.float32)

    def to_blocks(a):
        return jnp.moveaxis(a.reshape(B, nb, Q_BLOCK, *a.shape[2:]), 1, 0)

    def one_block(args):
        blk, qb, qib, wb = args
        q_pos = blk * Q_BLOCK + jnp.arange(Q_BLOCK)
        logits = jnp.einsum('bqhd,bsd->bqhs', qib.astype(jnp.float32), kf_idx) * (IDX_DIM ** -0.5)
        score = jnp.einsum('bqhs,bqh->bqs', jax.nn.relu(logits),
                           wb.astype(jnp.float32)) * (N_IDX_HEADS ** -0.5)
        causal = key_pos[None, :] <= q_pos[:, None]
        score = jnp.where(causal[None], score, -jnp.inf)
        _, sel = lax.top_k(score, k_top)
        valid = sel <= q_pos[None, :, None]
        k_sel = jax.vmap(lambda kk, ii: kk[ii])(k, sel)
        v_sel = jax.vmap(lambda vv, ii: vv[ii])(v, sel)
        qg = qb.reshape(B, Q_BLOCK, N_KV_HEADS, grp, HEAD_DIM).astype(jnp.float32)
        s = jnp.einsum('bqhgd,bqkhd->bqhgk', qg, k_sel.astype(jnp.float32)) * (HEAD_DIM ** -0.5)
        s = jnp.where(valid[:, :, None, None, :], s, -jnp.inf)
        p = jax.nn.softmax(s, axis=-1)
        o = jnp.einsum('bqhgk,bqkhd->bqhgd', p, v_sel.astype(jnp.float32))
        return o.reshape(B, Q_BLOCK, ATTN_WIDTH).astype(q.dtype)

    out = lax.map(one_block, (jnp.arange(nb), to_blocks(q), to_blocks(q_idx), to_blocks(w_idx)))
    return jnp.moveaxis(out, 0, 1).reshape(B, S, ATTN_WIDTH)


def setup_inputs(seed: int = 0) -> dict:
    key = jax.random.key(seed)
    ks = jax.random.split(key, 12)
    f32 = jnp.float32
    x = jax.random.normal(ks[0], (BATCH, SEQ, D_MODEL), f32)
    c = jax.random.normal(ks[1], (BATCH, D_MODEL), f32)
    norm_w = 1.0 + 0.1 * jax.random.normal(ks[2], (DEPTH, D_MODEL), f32)
    w_ada = jax.random.normal(ks[3], (DEPTH, D_MODEL, 3 * D_MODEL), f32) * D_MODEL ** -0.5
    b_ada = 0.01 * jax.random.normal(ks[4], (DEPTH, 3 * D_MODEL), f32)
    w_in = jax.random.normal(ks[5], (DEPTH, D_MODEL, D_IN_PROJ), f32) * D_MODEL ** -0.5
    q_norm_w = 1.0 + 0.1 * jax.random.normal(ks[6], (DEPTH, HEAD_DIM), f32)
    k_norm_w = 1.0 + 0.1 * jax.random.normal(ks[7], (DEPTH, HEAD_DIM), f32)
    w_pool = jax.random.normal(ks[8], (DEPTH, len(POOL_WINDOWS), POOL_GROUP, POOL_GROUP), f32) * POOL_GROUP ** -0.5
    pool_scale = 1.0 + 0.1 * jax.random.normal(ks[9], (DEPTH, POOL_WIDTH), f32)
    w_out = jax.random.normal(ks[10], (DEPTH, D_MIX, D_MODEL), f32) * D_MIX ** -0.5
    return {"x": x, "c": c, "norm_w": norm_w, "w_ada": w_ada, "b_ada": b_ada,
            "w_in": w_in, "q_norm_w": q_norm_w, "k_norm_w": k_norm_w,
            "w_pool": w_pool, "pool_scale": pool_scale, "w_out": w_out}


def reference(x, c, norm_w, w_ada, b_ada, w_in, q_norm_w, k_norm_w, w_pool, pool_scale, w_out):
    B, S, _ = x.shape
    pos = jnp.arange(S)
    split_points = [int(p) for p in np.cumsum(SPLITS)[:-1]]
    for l in range(DEPTH):
        mod = jnp.einsum('bd,de->be', jax.nn.silu(c), w_ada[l]) + b_ada[l]
        shift, scale, gate = jnp.split(mod, 3, axis=-1)
        h = rms_norm(x, norm_w[l]) * (1.0 + scale[:, None, :]) + shift[:, None, :]
        proj = jnp.einsum('bsd,de->bse', h, w_in[l])
        u_pool, g_pool, q, k, v, g_attn, qi, ki, wi = jnp.split(proj, split_points, axis=-1)
        pool_out = pool_mixer(u_pool, w_pool[l], pool_scale[l]) * jax.nn.silu(g_pool)
        q = rope(rms_norm(q.reshape(B, S, N_HEADS, HEAD_DIM), q_norm_w[l]), pos)
        k = rope(rms_norm(k.reshape(B, S, N_KV_HEADS, HEAD_DIM), k_norm_w[l]), pos)
        v = v.reshape(B, S, N_KV_HEADS, HEAD_DIM)
        qi = rope(qi.reshape(B, S, N_IDX_HEADS, IDX_DIM), pos)
        ki = rope(ki[:, :, None, :], pos)[:, :, 0, :]
        attn_out = dsa_attention(q, k, v, qi, ki, wi) * jax.nn.silu(g_attn)
        y = jnp.einsum('bse,ed->bsd', jnp.concatenate([pool_out, attn_out], axis=-1), w_out[l])
        x = x + gate[:, None, :] * y
    return x
```

```python
import numpy as np
import ml_dtypes
from contextlib import ExitStack
import concourse.bass as bass
import concourse.mybir as mybir
from concourse.bass_utils import run_bass_kernel_spmd

F32 = mybir.dt.float32
BF16 = mybir.dt.bfloat16
ALU = mybir.AluOpType
AF = mybir.ActivationFunctionType
AX = mybir.AxisListType

N_DMA_SEMS = 20


class Buf:
    __slots__ = ("name", "w", "r")

    def __init__(self, name):
        self.name = name
        self.w = None
        self.r = []


class _Rec:
    def __init__(self):
        self.call = None

    def __getattr__(self, name):
        def f(*a, **k):
            self.call = (name, a, k)
            return self
        return f


class Prog:
    ENGS = ("pe", "act", "dve", "pool", "sp")

    def __init__(self, nc):
        self.nc = nc
        self.stack = ExitStack()
        self.sems = {}
        self.cnt = {}
        self.floor = {}
        self.seen = {e: {} for e in self.ENGS}
        self.ops = {e: [] for e in self.ENGS}
        self.pending = {e: [] for e in self.ENGS}
        self.dma_rr = 0
        self.dma_last = {}
        self.out_toks = []
        self.started = False

    def begin(self):
        nc = self.nc
        for e in self.ENGS:
            self.sems[e] = self.stack.enter_context(nc.semaphore("sem_" + e))
            self.cnt[e] = 0
            self.floor[e] = 0
        for i in range(N_DMA_SEMS):
            k = "dma%d" % i
            self.sems[k] = self.stack.enter_context(nc.semaphore("sem_" + k))
            self.cnt[k] = 0
        self.started = True

    def buf(self, name="b"):
        return Buf(name)

    def bufs(self, name, n):
        return [Buf("%s%d" % (name, i)) for i in range(n)]

    def _engobj(self, blk_engine):
        return blk_engine

    def _collect(self, eng, reads, writes):
        waits = {}

        def need(tok):
            if tok is None:
                return
            key, c = tok
            if key in self.floor and c <= self.floor[key]:
                return
            if self.seen[eng].get(key, 0) >= c:
                return
            if waits.get(key, 0) < c:
                waits[key] = c

        for b in reads:
            if b.w is not None:
                if b.w[0] == eng and eng == "pe":
                    continue
                need(b.w)
        for b in writes:
            if b.w is not None and (b.w[0] != eng or eng != "pe"):
                need(b.w)
            for t in b.r:
                if t[0] != eng or eng != "pe":
                    need(t)
        for k, c in waits.items():
            self.seen[eng][k] = c
        return list(waits.items())

    def _check_pending(self, eng, reads, writes):
        for e2 in self.ENGS:
            if e2 == eng:
                continue
            for (r2, w2) in self.pending[e2]:
                for b in reads:
                    if any(b is x for x in w2):
                        raise RuntimeError("dep on unsignaled op: %s" % b.name)
                for b in writes:
                    if any(b is x for x in w2) or any(b is x for x in r2):
                        raise RuntimeError("dep on unsignaled op: %s" % b.name)

    def op(self, eng, fn, reads=(), writes=(), signal=True):
        reads = list(reads)
        writes = list(writes)
        self._check_pending(eng, reads, writes)
        waits = self._collect(eng, reads, writes)
        rec = _Rec()
        fn(rec)
        name_, a_, k_ = rec.call
        fn = (lambda e, name_=name_, a_=a_, k_=k_: getattr(e, name_)(*a_, **k_))
        self.ops[eng].append((waits, fn, self.sems[eng] if signal else None, 1))
        if signal:
            self.cnt[eng] += 1
            tok = (eng, self.cnt[eng])
            for (r2, w2) in self.pending[eng]:
                for b in w2:
                    b.w = tok
                    b.r = []
                for b in r2:
                    b.r.append(tok)
            self.pending[eng] = []
            for b in writes:
                b.w = tok
                b.r = []
            for b in reads:
                b.r.append(tok)
            return tok
        else:
            self.pending[eng].append((reads, writes))
            return None

    def dma(self, q, out, in_, reads=(), writes=(), out_final=False, **kw):
        reads = list(reads)
        writes = list(writes)
        self._check_pending(q, reads, writes)
        if q == "pool":
            key = "swdma%d" % len([k for k in self.sems if k.startswith("swdma")])
            self.sems[key] = self.stack.enter_context(self.nc.semaphore("sem_" + key))
            self.cnt[key] = 0
        else:
            key = "dma%d" % self.dma_rr
            self.dma_rr = (self.dma_rr + 1) % N_DMA_SEMS
        waits = self._collect(q, reads, writes)
        prev = self.dma_last.get(key)
        if prev is not None and self.seen[q].get(key, 0) < prev[1]:
            waits.append((key, prev[1]))
            self.seen[q][key] = prev[1]
        self.cnt[key] += 16
        tok = (key, self.cnt[key])
        self.dma_last[key] = tok
        fn = (lambda e, o=out, i=in_, kw=kw: e.dma_start(out=o, in_=i, **kw))
        self.ops[q].append((waits, fn, self.sems[key], 16))
        for b in writes:
            b.w = tok
            b.r = []
        for b in reads:
            b.r.append(tok)
        if out_final:
            self.out_toks.append(tok)
        return tok

    def flush(self, final=False):
        nc = self.nc
        for e in self.ENGS:
            if self.pending[e]:
                raise RuntimeError("pending unsignaled ops on %s at flush" % e)
        if final:
            ws = {}
            for (k, c) in self.out_toks:
                ws[k] = max(ws.get(k, 0), c)
            self.ops["sp"].append((list(ws.items()), None, None, 0))
        sems = self.sems

        def body(ops):
            def run(e):
                for (waits, fn, sem, inc) in ops:
                    for (k, c) in waits:
                        e.wait_ge(sems[k], c)
                    if fn is None:
                        continue
                    ins = fn(e)
                    if sem is not None:
                        ins.then_inc(sem, inc)
            return run

        with nc.Block() as blk:
            m = {"pe": blk.tensor, "act": blk.scalar, "dve": blk.vector, "pool": blk.gpsimd, "sp": blk.sync}
            for e in self.ENGS:
                if self.ops[e]:
                    m[e](body(self.ops[e]))
        for e in self.ENGS:
            self.ops[e] = []
            self.floor[e] = self.cnt[e]

    def end(self):
        self.flush(final=True)
        self.stack.close()


S = 2048
D = 1024
NT = 16
EPS = 1e-6
KBIS = 10
C_U, C_GP, C_Q, C_K, C_V, C_GA, C_QI, C_KI, C_WI = 0, 512, 1024, 1536, 1664, 1792, 2304, 2816, 2880
NEGM = 30000.0


def host_consts():
    half = 32
    freqs = (10000.0 ** (-np.arange(half, dtype=np.float32) / half)).astype(np.float32)
    pos = np.arange(S, dtype=np.float32)
    ang = pos[:, None] * freqs[None, :]
    cos = np.cos(ang).astype(np.float32)
    sin = np.sin(ang).astype(np.float32)
    cos2 = np.concatenate([cos, cos], axis=1)
    sin2 = np.concatenate([-sin, sin], axis=1)
    cs = np.stack([cos2, sin2], axis=1)
    cs = cs.reshape(NT, 128, 2, 64).transpose(1, 0, 2, 3)
    t = np.arange(128)
    causal = np.where(t[None, :] <= t[:, None], 0.0, -1e30).astype(np.float32)
    pinv = np.zeros((128, 4, 16), np.float32)
    for g, w in enumerate((2, 4, 8, 16)):
        pinv[:, g, :] = 1.0 / np.minimum(np.arange(1, 17), w)
    pow2 = np.tile((2.0 ** -(np.arange(KBIS + 1) + 1.0)).astype(np.float32)[None, :], (128, 1))
    return {
        "cs": np.ascontiguousarray(cs),
        "ident": np.eye(128).astype(ml_dtypes.bfloat16),
        "identf": np.eye(128).astype(np.float32),
        "causal": causal,
        "pinv": pinv,
        "pow2": pow2,
    }


def build(stage=99):
    nc = bass.Bass("TRN2", target_bir_lowering=False)
    dt = nc.dram_tensor

    def din(name, shape, dtype=F32):
        return dt(name, list(shape), dtype, kind="ExternalInput").ap()

    x_d = din("x", [S, D])
    ccol_d = din("c_col", [128, 8])
    nw_d = din("nw_col", [128, 8])
    bada_d = din("bada_col", [128, 16])
    bgate_d = din("bgate_row", [1, 1024])
    wada_d = din("w_ada", [D, 3 * D])
    win_d = din("w_in", [D, 2888])
    wout_d = din("w_out", [D, D])
    wpool_d = din("w_pool", [128, 4, 128])
    pscale_d = din("pscale_col", [128, 4])
    qnw_d = din("qnw_b", [128, 64])
    knw_d = din("knw_b", [128, 64])
    cs_d = din("cs", [128, NT, 2, 64])
    ident_d = din("ident", [128, 128], BF16)
    identf_d = din("identf", [128, 128])
    causal_d = din("causal", [128, 128])
    pinv_d = din("pinv", [128, 4, 16])
    pow2_d = din("pow2", [128, KBIS + 1])
    out_d = dt("out", [S, D], F32, kind="ExternalOutput").ap()
    dbg_d = {}

    P = Prog(nc)
    G = ExitStack()

    def sb(name, shape, dtype=F32, stack=None):
        return (stack or G).enter_context(nc.sbuf_tensor("s_" + name, list(shape), dtype))

    def ps(name, shape, dtype=F32, stack=None):
        return (stack or G).enter_context(nc.psum_tensor("p_" + name, list(shape), dtype))

    def dump(name, ap, shape, bufs, dtype=F32):
        d = dt("dbg_" + name, list(shape), dtype, kind="ExternalOutput").ap()
        P.dma("sp", d, ap, reads=bufs, out_final=True)

    with G:
        P.begin()
        ident = sb("ident", [128, 128], BF16)
        causal = sb("causal", [128, 128])
        pow2 = sb("pow2", [128, KBIS + 1])
        GT = sb("GT", [128, 8])
        shT = sb("shT", [128, 8])
        gate_b = sb("gate_b", [128, 1024])
        aT = sb("aT", [128, 8, S], BF16)
        QT = sb("QT", [128, 4, S], BF16)
        KT = sb("KT", [128, 4, S], BF16)
        QIT = sb("QIT", [128, 4, S], BF16)
        KIT = sb("KIT", [128, 2, S], BF16)
        Vx = sb("Vx", [128, NT, 2, 65], BF16)
        sga = sb("sga", [128, NT, 512], BF16)
        wi_all = sb("wi_all", [128, NT, 8])
        thr0 = sb("thr0", [128, 1])
        b_ident, b_identf, b_causal, b_pow2 = P.buf("ident"), P.buf("identf"), P.buf("causal"), P.buf("pow2")
        b_GT, b_shT, b_gate = P.buf("GT"), P.buf("shT"), P.buf("gate")
        b_aT = P.bufs("aT", 8 * 4)
        b_Vx = P.buf("Vx")
        P.dma("sp", ident[:], ident_d, writes=[b_ident])
        P.dma("sp", causal[:], causal_d, writes=[b_causal])
        P.dma("sp", pow2[:], pow2_d, writes=[b_pow2])
        b_thr0 = P.buf("thr0")
        P.op("pool", lambda e: e.memset(thr0[:], -1e29), writes=[b_thr0])
        P.op("pool", lambda e: e.memset(Vx[:], 1.0), writes=[b_Vx])

        with ExitStack() as L:
            wa = [sb("wa%d" % i, [128, 8, 512], F32, L) for i in range(3)]
            b_wa = P.bufs("wa", 3)
            ccol = sb("ccol", [128, 8], F32, L)
            sccol = sb("sccol", [128, 8], F32, L)
            nwcol = sb("nwcol", [128, 8], F32, L)
            badac = sb("badac", [128, 16], F32, L)
            bgate = sb("bgate", [1, 1024], F32, L)
            modT = sb("modT", [128, 16], F32, L)
            grow = sb("grow", [1, 1024], F32, L)
            ones_r = sb("ones_r", [1, 128], F32, L)
            psT = ps("psT", [128, 512], F32, L)
            psG = [ps("psG%d" % i, [128, 512], F32, L) for i in range(2)]
            b_cc, b_scc, b_nw, b_bada, b_bg, b_modT, b_grow, b_ones, b_psT = [P.buf(n) for n in
                "cc scc nw bada bg modT grow ones psT".split()]
            b_psG = P.bufs("psG", 2)
            P.dma("sp", ccol[:], ccol_d, writes=[b_cc])
            P.dma("sp", nwcol[:], nw_d, writes=[b_nw])
            P.dma("sp", badac[:], bada_d, writes=[b_bada])
            P.dma("sp", bgate[:], bgate_d, writes=[b_bg])
            P.op("pool", lambda e: e.memset(ones_r[:], 1.0), writes=[b_ones])
            P.op("act", lambda e: e.activation(out=sccol[:], in_=ccol[:], func=AF.Silu), reads=[b_cc], writes=[b_scc])
            wada_v = wada_d.rearrange("(kc p) e -> p kc e", p=128)
            for cb in range(6):
                wb_, bb_ = wa[cb % 3], b_wa[cb % 3]
                P.dma("sp" if cb % 2 == 0 else "act", wb_[:], wada_v[:, :, cb * 512:(cb + 1) * 512], writes=[bb_])
                if cb < 4:
                    for jb in range(4):
                        col = cb * 4 + jb
                        for kc in range(8):
                            P.op("pe", lambda e, wb_=wb_, kc=kc, jb=jb, col=col: e.matmul(
                                out=psT[:, col:col + 1], lhsT=wb_[:, kc, jb * 128:(jb + 1) * 128],
                                rhs=sccol[:, kc:kc + 1], start=(kc == 0), stop=(kc == 7)),
                                reads=[bb_, b_scc], writes=[b_psT], signal=(kc == 7))
                else:
                    hb = cb - 4
                    for kc in range(8):
                        P.op("pe", lambda e, wb_=wb_, kc=kc, hb=hb: e.matmul(
                            out=psG[hb][0:1, :], lhsT=sccol[:, kc:kc + 1], rhs=wb_[:, kc, :],
                            start=(kc == 0), stop=(kc == 7)),
                            reads=[bb_, b_scc], writes=[b_psG[hb]], signal=(kc == 7))
                    P.op("dve", lambda e, hb=hb: e.tensor_tensor(out=grow[0:1, hb * 512:(hb + 1) * 512], in0=psG[hb][0:1, :],
                                                                in1=bgate[0:1, hb * 512:(hb + 1) * 512], op=ALU.add),
                         reads=[b_psG[hb], b_bg], writes=[b_grow])
            P.op("dve", lambda e: e.tensor_tensor(out=modT[:], in0=psT[:, 0:16], in1=badac[:], op=ALU.add),
                 reads=[b_psT, b_bada], writes=[b_modT])
            P.op("dve", lambda e: e.scalar_tensor_tensor(out=GT[:], in0=modT[:, 8:16], scalar=1.0, in1=nwcol[:],
                                                         op0=ALU.add, op1=ALU.mult), reads=[b_modT, b_nw], writes=[b_GT])
            P.op("dve", lambda e: e.tensor_copy(out=shT[:], in_=modT[:, 0:8]), reads=[b_modT], writes=[b_shT])
            for hb in range(2):
                P.op("pe", lambda e, hb=hb: e.matmul(out=psG[hb][:, :], lhsT=ones_r[0:1, :], rhs=grow[0:1, hb * 512:(hb + 1) * 512],
                                                     start=True, stop=True), reads=[b_ones, b_grow], writes=[b_psG[hb]])
                P.op("act", lambda e, hb=hb: e.activation(out=gate_b[:, hb * 512:(hb + 1) * 512], in_=psG[hb][:, :], func=AF.Copy),
                     reads=[b_psG[hb]], writes=[b_gate])
            if stage == 0:
                dump("GT", GT[:], [128, 8], [b_GT])
                dump("shT", shT[:], [128, 8], [b_shT])
                dump("gate_b", gate_b[:], [128, 1024], [b_gate])
                P.end()
                return nc
            P.flush()

        with ExitStack() as H:
            hT = sb("hT", [128, 8, S], BF16, H)
            b_hT = P.bufs("hT", NT)
            with ExitStack() as L:
                identf = sb("identf", [128, 128], F32, L)
                P.dma("sp", identf[:], identf_d, writes=[b_identf])
                xt = [sb("xt%d" % i, [128, 1024], F32, L) for i in range(4)]
                xn = [sb("xn%d" % i, [128, 1024], F32, L) for i in range(2)]
                junk = sb("junk1", [128, 1024], BF16, L)
                tmpf = [sb("tmpf%d" % i, [128, 8, 128], F32, L) for i in range(2)]
                ss = sb("ss", [128, NT], F32, L)
                rstd = sb("rstd", [128, NT], F32, L)
                psX = [ps("psX%d" % i, [128, 8, 128], F32, L) for i in range(2)]
                b_xt, b_xn, b_tmpf, b_psX = P.bufs("xt", 4), P.bufs("xn", 2), P.bufs("tmpf", 2), P.bufs("psX", 2)
                b_junk = P.buf("junk")
                b_ss = P.bufs("ss", NT)
                xv = x_d.rearrange("(n p) d -> n p d", p=128)
                def a1_stage0(i):
                    k4 = i % 4
                    P.dma("sp", xt[k4][:], xv[i], writes=[b_xt[k4]])
                    P.op("act", lambda e: e.activation(out=junk[:], in_=xt[k4][:], func=AF.Square, accum_out=ss[:, i:i + 1]),
                         reads=[b_xt[k4]], writes=[b_junk, b_ss[i]])

                def a1_stage1(i):
                    k = i % 2
                    k4 = i % 4
                    P.op("dve", lambda e: e.tensor_scalar(out=ss[:, i:i + 1], in0=ss[:, i:i + 1], scalar1=1.0 / D, scalar2=EPS,
                                                          op0=ALU.mult, op1=ALU.add), reads=[b_ss[i]], writes=[b_ss[i]])
                    P.op("act", lambda e: e.activation(out=ss[:, i:i + 1], in_=ss[:, i:i + 1], func=AF.Sqrt),
                         reads=[b_ss[i]], writes=[b_ss[i]])
                    P.op("dve", lambda e: e.reciprocal(out=rstd[:, i:i + 1], in_=ss[:, i:i + 1]), reads=[b_ss[i]], writes=[b_ss[i]])
                    P.op("act", lambda e: e.activation(out=xn[k][:], in_=xt[k4][:], func=AF.Copy, scale=rstd[:, i:i + 1]),
                         reads=[b_xt[k4], b_ss[i]], writes=[b_xn[k]])

                def a1_stage2(i):
                    k = i % 2
                    for kc in range(8):
                        P.op("pe", lambda e: e.transpose(out=psX[k][:, kc, :], in_=xn[k][:, kc * 128:(kc + 1) * 128], identity=identf[:]),
                             reads=[b_xn[k], b_identf], writes=[b_psX[k]], signal=(kc == 7))
                    P.op("dve", lambda e: e.tensor_tensor(out=tmpf[k][:], in0=psX[k][:], in1=GT[:].unsqueeze(2).to_broadcast([128, 8, 128]),
                                                          op=ALU.mult), reads=[b_psX[k], b_GT], writes=[b_tmpf[k]])
                    P.op("pool", lambda e: e.tensor_tensor(out=hT[:, :, i * 128:(i + 1) * 128], in0=tmpf[k][:],
                                                           in1=shT[:].unsqueeze(2).to_broadcast([128, 8, 128]), op=ALU.add),
                         reads=[b_tmpf[k], b_shT], writes=[b_hT[i]])

                for i_ in range(NT + 2):
                    if i_ < NT:
                        a1_stage0(i_)
                    if 1 <= i_ < NT + 1:
                        a1_stage1(i_ - 1)
                    if i_ >= 2:
                        a1_stage2(i_ - 2)
                if stage == 1:
                    dump("hT", hT[:], [128, 8, S], b_hT, BF16)
                    P.end()
                    return nc
                P.flush()

            with ExitStack() as L:
                NA = 2888 - C_Q
                wA = sb("wA", [128, 8, NA], BF16, L)
                b_wA = P.bufs("wA", 4)
                win_v = win_d.rearrange("(kc p) e -> p kc e", p=128)
                bounds = [0, 512, 768, 1280, NA]
                for j in range(4):
                    P.dma("pool", wA[:, :, bounds[j]:bounds[j + 1]], win_v[:, :, C_Q + bounds[j]:C_Q + bounds[j + 1]], writes=[b_wA[j]])
                cs = sb("cs", [128, NT, 2, 64], F32, L)
                qnw = sb("qnw", [128, 64], F32, L)
                knw = sb("knw", [128, 64], F32, L)
                b_cs, b_qnw, b_knw = P.buf("cs"), P.buf("qnw"), P.buf("knw")
                P.dma("sp", cs[:], cs_d, writes=[b_cs])
                P.dma("sp", qnw[:], qnw_d, writes=[b_qnw])
                P.dma("sp", knw[:], knw_d, writes=[b_knw])
                psQ = ps("psQ", [128, 512], F32, L)
                psKV = ps("psKV", [128, 512], F32, L)
                psGA = ps("psGA", [128, 512], F32, L)
                psQI = ps("psQI", [128, 512], F32, L)
                psTR = [ps("psTR%d" % i, [128, 8, 128], BF16, L) for i in range(2)]
                b_psQ, b_psKV, b_psGA, b_psQI = P.buf("psQ"), P.buf("psKV"), P.buf("psGA"), P.buf("psQI")
                b_psTR = P.bufs("psTR", 2)
                q_sb_R = [sb("q_sb%d" % r_, [128, 8, 64], F32, L) for r_ in range(2)]
                qi_sb_R = [sb("qi_sb%d" % r_, [128, 8, 64], F32, L) for r_ in range(2)]
                k_sb_R = [sb("k_sb%d" % r_, [128, 2, 64], F32, L) for r_ in range(2)]
                ki_sb_R = [sb("ki_sb%d" % r_, [128, 1, 64], F32, L) for r_ in range(2)]
                sq = sb("sq", [128, 8, 64], F32, L)
                ssq_R = [sb("ssq%d" % r_, [128, 8], F32, L) for r_ in range(2)]
                kss_R = [sb("kss%d" % r_, [128, 2], F32, L) for r_ in range(2)]
                qg = sb("qg", [128, 8, 64], F32, L)
                kg = sb("kg", [128, 2, 64], F32, L)
                kt1 = sb("kt1", [128, 2, 64], F32, L)
                kt2 = sb("kt2", [128, 2, 64], F32, L)
                ksq = kt2
                it2 = sb("it2", [128, 8, 64], F32, L)
                jt2 = sb("jt2", [128, 1, 64], F32, L)
                qr_R = [sb("qr%d" % r_, [128, 8, 64], BF16, L) for r_ in range(2)]
                qir_R = [sb("qir%d" % r_, [128, 8, 64], BF16, L) for r_ in range(2)]
                krd_R = [sb("krd%d" % r_, [128, 2, 2, 128], BF16, L) for r_ in range(2)]
                kird_R = [sb("kird%d" % r_, [128, 2, 128], BF16, L) for r_ in range(2)]
                names = "q_sb qi_sb k_sb ki_sb sq ssq kss qg kg kt1 kt2 it2 jt2 qr qir krd kird".split()
                bb0 = {n: P.buf(n) for n in names}
                bb0["ksq"] = bb0["kt2"]
                RING = "q_sb qi_sb k_sb ki_sb qr qir krd kird ssq kss".split()
                bbR = [{n: P.buf(n + str(r_)) for n in RING} for r_ in range(2)]
                bb = dict(bb0)
                b_QT, b_KT, b_QIT, b_KIT, b_sga, b_wi = [P.bufs(n, NT) for n in "QT KT QIT KIT sga wi".split()]
                for r_ in range(2):
                    P.op("pool", lambda e: e.memset(krd_R[r_][:], 0.0), writes=[bbR[r_]["krd"]])
                    P.op("pool", lambda e: e.memset(kird_R[r_][:], 0.0), writes=[bbR[r_]["kird"]])

                def bc(ap, n):
                    return ap.unsqueeze(1).to_broadcast([128, n, 64])

                def rope(eng, src, nsrc, nh, i, ta, tb, na, nb, bb):
                    cosb = bc(cs[:, i, 0, :], nh)
                    P.op(eng, lambda e: e.tensor_tensor(out=tb[:, :, 0:32], in0=src[:, :, 32:64],
                                                        in1=cs[:, i, 1, 0:32].unsqueeze(1).to_broadcast([128, nh, 32]), op=ALU.mult),
                         reads=[bb[nsrc], b_cs], writes=[bb[nb]])
                    P.op(eng, lambda e: e.tensor_tensor(out=tb[:, :, 32:64], in0=src[:, :, 0:32],
                                                        in1=cs[:, i, 1, 32:64].unsqueeze(1).to_broadcast([128, nh, 32]), op=ALU.mult),
                         reads=[bb[nsrc], b_cs], writes=[bb[nb]])
                    P.op(eng, lambda e: e.tensor_tensor(out=ta[:], in0=src[:], in1=cosb, op=ALU.mult),
                         reads=[bb[nsrc], b_cs], writes=[bb[na]])

                def compute_a(i):
                    rr = i % 2
                    q_sb, qi_sb, k_sb, ki_sb, qr, qir, krd, kird = (q_sb_R[rr], qi_sb_R[rr], k_sb_R[rr], ki_sb_R[rr], qr_R[rr], qir_R[rr], krd_R[rr], kird_R[rr])
                    ssq, kss = ssq_R[rr], kss_R[rr]
                    bb = dict(bb0)
                    bb.update(bbR[rr])
                    tk = slice(i * 128, (i + 1) * 128)
                    def proj(pst, bps, c0, c1, o0, wbufs):
                        for kc in range(8):
                            P.op("pe", lambda e, kc=kc: e.matmul(out=pst[:, o0:o0 + (c1 - c0)], lhsT=hT[:, kc, tk], rhs=wA[:, kc, c0:c1],
                                                                 start=(kc == 0), stop=(kc == 7)),
                                 reads=[b_hT[i]] + wbufs, writes=[bps], signal=(kc == 7))
                    proj(psQ, b_psQ, 0, 512, 0, [b_wA[0]])
                    P.op("act", lambda e: e.activation(out=q_sb[:].rearrange("p h d -> p (h d)"), in_=psQ[:, :], func=AF.Copy),
                         reads=[b_psQ], writes=[bb["q_sb"]])
                    proj(psKV, b_psKV, 512, 768, 0, [b_wA[1]])
                    proj(psKV, b_psKV, 1792, 1864, 256, [b_wA[3]])
                    P.op("act", lambda e: e.activation(out=k_sb[:].rearrange("p h d -> p (h d)"), in_=psKV[:, 0:128], func=AF.Copy),
                         reads=[b_psKV], writes=[bb["k_sb"]])
                    P.op("act", lambda e, i=i: e.activation(out=Vx[:, i, :, 0:64], in_=psKV[:, 128:256].rearrange("p (g d) -> p g d", g=2), func=AF.Copy),
                         reads=[b_psKV], writes=[b_Vx])
                    P.op("act", lambda e: e.activation(out=ki_sb[:].rearrange("p h d -> p (h d)"), in_=psKV[:, 256:320], func=AF.Copy),
                         reads=[b_psKV], writes=[bb["ki_sb"]])
                    P.op("act", lambda e, i=i: e.activation(out=wi_all[:, i, :], in_=psKV[:, 320:328], func=AF.Copy),
                         reads=[b_psKV], writes=[b_wi[i]])
                    proj(psGA, b_psGA, 768, 1280, 0, [b_wA[2]])
                    P.op("act", lambda e, i=i: e.activation(out=sga[:, i, :], in_=psGA[:, :], func=AF.Silu),
                         reads=[b_psGA], writes=[b_sga[i]])
                    proj(psQI, b_psQI, 1280, 1792, 0, [b_wA[3]])
                    P.op("act", lambda e: e.activation(out=qi_sb[:].rearrange("p h d -> p (h d)"), in_=psQI[:, :], func=AF.Copy),
                         reads=[b_psQI], writes=[bb["qi_sb"]])
                    P.op("dve", lambda e: e.tensor_tensor(out=sq[:], in0=q_sb[:], in1=q_sb[:], op=ALU.mult), reads=[bb["q_sb"]], writes=[bb["sq"]])
                    P.op("dve", lambda e: e.tensor_reduce(out=ssq[:], in_=sq[:], axis=AX.X, op=ALU.add), reads=[bb["sq"]], writes=[bb["ssq"]])
                    P.op("dve", lambda e: e.tensor_scalar(out=ssq[:], in0=ssq[:], scalar1=1.0 / 64, scalar2=EPS, op0=ALU.mult, op1=ALU.add),
                         reads=[bb["ssq"]], writes=[bb["ssq"]])
                    P.op("dve", lambda e: e.tensor_tensor(out=ksq[:], in0=k_sb[:], in1=k_sb[:], op=ALU.mult), reads=[bb["k_sb"]], writes=[bb["ksq"]])
                    P.op("dve", lambda e: e.tensor_reduce(out=kss[:], in_=ksq[:], axis=AX.X, op=ALU.add), reads=[bb["ksq"]], writes=[bb["kss"]])
                    P.op("dve", lambda e: e.tensor_scalar(out=kss[:], in0=kss[:], scalar1=1.0 / 64, scalar2=EPS, op0=ALU.mult, op1=ALU.add),
                         reads=[bb["kss"]], writes=[bb["kss"]])
                    rope("pool", qi_sb, "qi_sb", 8, i, qi_sb, it2, "qi_sb", "it2", bb)
                    P.op("pool", lambda e: e.tensor_tensor(out=qir[:], in0=qi_sb[:], in1=it2[:], op=ALU.add), reads=[bb["qi_sb"], bb["it2"]], writes=[bb["qir"]])
                    rope("pool", ki_sb, "ki_sb", 1, i, ki_sb, jt2, "ki_sb", "jt2", bb)
                    for dup in range(2):
                        P.op("pool", lambda e, dup=dup: e.tensor_tensor(out=kird[:, dup:dup + 1, dup * 64:(dup + 1) * 64], in0=ki_sb[:], in1=jt2[:], op=ALU.add),
                             reads=[bb["ki_sb"], bb["jt2"]], writes=[bb["kird"]])

                def compute_b(i):
                    rr = i % 2
                    q_sb, qi_sb, k_sb, ki_sb, qr, qir, krd, kird = (q_sb_R[rr], qi_sb_R[rr], k_sb_R[rr], ki_sb_R[rr], qr_R[rr], qir_R[rr], krd_R[rr], kird_R[rr])
                    ssq, kss = ssq_R[rr], kss_R[rr]
                    bb = dict(bb0)
                    bb.update(bbR[rr])
                    tk = slice(i * 128, (i + 1) * 128)
                    P.op("act", lambda e: e.activation(out=ssq[:], in_=ssq[:], func=AF.Sqrt), reads=[bb["ssq"]], writes=[bb["ssq"]])
                    P.op("dve", lambda e: e.reciprocal(out=ssq[:], in_=ssq[:]), reads=[bb["ssq"]], writes=[bb["ssq"]])
                    P.op("dve", lambda e: e.tensor_tensor(out=qg[:], in0=q_sb[:], in1=bc(qnw[:], 8), op=ALU.mult),
                         reads=[bb["q_sb"], b_qnw], writes=[bb["qg"]])
                    rope("dve", qg, "qg", 8, i, qg, sq, "qg", "sq", bb)
                    P.op("dve", lambda e: e.tensor_tensor(out=qg[:], in0=qg[:], in1=sq[:], op=ALU.add), reads=[bb["qg"], bb["sq"]], writes=[bb["qg"]])
                    P.op("dve", lambda e: e.tensor_tensor(out=qr[:], in0=qg[:], in1=ssq[:].unsqueeze(2).to_broadcast([128, 8, 64]), op=ALU.mult),
                         reads=[bb["qg"], bb["ssq"]], writes=[bb["qr"]])
                    P.op("act", lambda e: e.activation(out=kss[:], in_=kss[:], func=AF.Sqrt), reads=[bb["kss"]], writes=[bb["kss"]])
                    P.op("dve", lambda e: e.reciprocal(out=kss[:], in_=kss[:]), reads=[bb["kss"]], writes=[bb["kss"]])
                    P.op("dve", lambda e: e.tensor_tensor(out=kg[:], in0=k_sb[:], in1=bc(knw[:], 2), op=ALU.mult),
                         reads=[bb["k_sb"], b_knw], writes=[bb["kg"]])
                    rope("dve", kg, "kg", 2, i, kt1, kt2, "kt1", "kt2", bb)
                    P.op("dve", lambda e: e.tensor_tensor(out=kt1[:], in0=kt1[:], in1=kt2[:], op=ALU.add), reads=[bb["kt1"], bb["kt2"]], writes=[bb["kt1"]])
                    for dup in range(2):
                        P.op("dve", lambda e, dup=dup: e.tensor_tensor(out=krd[:, :, dup, dup * 64:(dup + 1) * 64], in0=kt1[:], in1=kss[:].unsqueeze(2).to_broadcast([128, 2, 64]),
                                                                        op=ALU.mult), reads=[bb["kt1"], bb["kss"]], writes=[bb["krd"]])

                def transpose_tile(i):
                    rr = i % 2
                    q_sb, qi_sb, k_sb, ki_sb, qr, qir, krd, kird = (q_sb_R[rr], qi_sb_R[rr], k_sb_R[rr], ki_sb_R[rr], qr_R[rr], qir_R[rr], krd_R[rr], kird_R[rr])
                    ssq, kss = ssq_R[rr], kss_R[rr]
                    bb = dict(bb0)
                    bb.update(bbR[rr])
                    tk = slice(i * 128, (i + 1) * 128)
                    pa, pb_ = psTR[0], psTR[1]
                    qrf = qr[:].rearrange("p h d -> p (h d)")
                    qirf = qir[:].rearrange("p h d -> p (h d)")
                    for blk in range(4):
                        P.op("pe", lambda e, blk=blk: e.transpose(out=pa[:, blk, :], in_=qrf[:, blk * 128:(blk + 1) * 128], identity=ident[:]),
                             reads=[bb["qr"], b_ident], writes=[b_psTR[0]], signal=False)
                    for gp in range(4):
                        P.op("pe", lambda e, gp=gp: e.transpose(out=pa[:, 4 + gp, :], in_=krd[:, gp // 2, gp % 2, :], identity=ident[:]),
                             reads=[bb["krd"], b_ident], writes=[b_psTR[0]], signal=(gp == 3))
                    for blk in range(4):
                        P.op("pe", lambda e, blk=blk: e.transpose(out=pb_[:, blk, :], in_=qirf[:, blk * 128:(blk + 1) * 128], identity=ident[:]),
                             reads=[bb["qir"], b_ident], writes=[b_psTR[1]], signal=False)
                    for par in range(2):
                        P.op("pe", lambda e, par=par: e.transpose(out=pb_[:, 4 + par, :], in_=kird[:, par, :], identity=ident[:]),
                             reads=[bb["kird"], b_ident], writes=[b_psTR[1]], signal=(par == 1))
                    P.op("dve", lambda e: e.tensor_copy(out=QT[:, :, tk], in_=pa[:, 0:4, :]), reads=[b_psTR[0]], writes=[b_QT[i]])
                    P.op("dve", lambda e: e.tensor_copy(out=KT[:, :, tk], in_=pa[:, 4:8, :]), reads=[b_psTR[0]], writes=[b_KT[i]])
                    P.op("act", lambda e: e.activation(out=QIT[:, :, tk], in_=pb_[:, 0:4, :], func=AF.Copy), reads=[b_psTR[1]], writes=[b_QIT[i]])
                    P.op("act", lambda e: e.activation(out=KIT[:, :, tk], in_=pb_[:, 4:6, :], func=AF.Copy), reads=[b_psTR[1]], writes=[b_KIT[i]])

                for i_ in range(NT + 2):
                    if 1 <= i_ < NT + 1:
                        compute_b(i_ - 1)
                    if i_ >= 2:
                        transpose_tile(i_ - 2)
                    if i_ < NT:
                        compute_a(i_)

                if stage == 2:
                    dump("QT", QT[:], [128, 4, S], b_QT, BF16)
                    dump("KT", KT[:], [128, 4, S], b_KT, BF16)
                    dump("QIT", QIT[:], [128, 4, S], b_QIT, BF16)
                    dump("KIT", KIT[:], [128, 2, S], b_KIT, BF16)
                    dump("Vx", Vx[:], [128, NT, 2, 65], [b_Vx], BF16)
                    dump("sga", sga[:], [128, NT, 512], b_sga, BF16)
                    dump("wi", wi_all[:], [128, NT, 8], b_wi)
                    P.end()
                    return nc
                P.flush()

            with ExitStack() as L:
                wP = sb("wP", [128, 8, 1024], BF16, L)
                b_wP = P.bufs("wP", 2)
                win_v = win_d.rearrange("(kc p) e -> p kc e", p=128)
                for j in range(2):
                    P.dma("pool", wP[:, :, j * 512:(j + 1) * 512], win_v[:, :, j * 512:(j + 1) * 512], writes=[b_wP[j]])
                wpl = sb("wpl", [128, 4, 128], BF16, L)
                psc = sb("psc", [128, 4], F32, L)
                pinv = sb("pinv", [128, 4, 16], F32, L)
                b_wpl, b_psc, b_pinv = P.buf("wpl"), P.buf("psc"), P.buf("pinv")
                P.dma("pool", wpl[:], wpool_d, writes=[b_wpl])
                P.dma("sp", psc[:], pscale_d, writes=[b_psc])
                P.dma("sp", pinv[:], pinv_d, writes=[b_pinv])
                uT = sb("uT", [128, 16 + S], F32, L)
                sA = sb("sA", [128, 16 + S], F32, L)
                sB = sb("sB", [128, 16 + S], F32, L)
                sgT = sb("sgT", [128, S], BF16, L)
                pooled = sb("pooled", [128, S], BF16, L)
                fix = sb("fix", [128, 16], F32, L)
                b_uT, b_sA, b_sB, b_sgT, b_pooled, b_fix = [P.buf(n) for n in "uT sA sB sgT pooled fix".split()]
                for t_, b_ in ((uT, b_uT), (sA, b_sA), (sB, b_sB)):
                    P.op("pool", lambda e, t_=t_: e.memset(t_[:, 0:16], 0.0), writes=[b_])
                psU = [ps("psU%d" % i, [128, 512], F32, L) for i in range(2)]
                psM = [ps("psM%d" % i, [128, 512], F32, L) for i in range(2)]
                b_psU, b_psM = P.bufs("psU", 2), P.bufs("psM", 2)
                cnt_u = 0
                for g in range(4):
                    for part in range(2):
                        c0 = part * 512 + g * 128
                        for Q in range(4):
                            u_ = cnt_u % 2
                            cnt_u += 1
                            for kc in range(8):
                                P.op("pe", lambda e, kc=kc: e.matmul(out=psU[u_][:, :], lhsT=wP[:, kc, c0:c0 + 128], rhs=hT[:, kc, Q * 512:(Q + 1) * 512],
                                                                     start=(kc == 0), stop=(kc == 7)),
                                     reads=b_hT[4 * Q:4 * Q + 4] + [b_wP[part]], writes=[b_psU[u_]], signal=(kc == 7))
                            if part == 0:
                                P.op("act", lambda e: e.activation(out=uT[:, 16 + Q * 512:16 + (Q + 1) * 512], in_=psU[u_][:, :], func=AF.Copy),
                                     reads=[b_psU[u_]], writes=[b_uT])
                            else:
                                P.op("act", lambda e: e.activation(out=sgT[:, Q * 512:(Q + 1) * 512], in_=psU[u_][:, :], func=AF.Silu),
                                     reads=[b_psU[u_]], writes=[b_sgT])
                    w_ = 2 ** (g + 1)
                    cur, bcur = uT, b_uT
                    ring = [(sA, b_sA), (sB, b_sB)]
                    for lv in range(g + 1):
                        sh = 2 ** lv
                        dst, bdst = ring[lv % 2]
                        P.op("dve", lambda e, cur=cur, dst=dst, sh=sh: e.tensor_tensor(out=dst[:, 16:16 + S], in0=cur[:, 16:16 + S],
                                                                                       in1=cur[:, 16 - sh:16 + S - sh], op=ALU.add),
                             reads=[bcur], writes=[bdst])
                        cur, bcur = dst, bdst
                    P.op("dve", lambda e, cur=cur: e.scalar_tensor_tensor(out=pooled[:], in0=cur[:, 16:16 + S], scalar=1.0 / w_, in1=uT[:, 16:16 + S],
                                                                          op0=ALU.mult, op1=ALU.subtract), reads=[bcur, b_uT], writes=[b_pooled])
                    P.op("dve", lambda e, cur=cur: e.tensor_tensor(out=fix[:], in0=cur[:, 16:32], in1=pinv[:, g, :], op=ALU.mult),
                         reads=[bcur, b_pinv], writes=[b_fix])
                    P.op("dve", lambda e: e.tensor_tensor(out=pooled[:, 0:16], in0=fix[:], in1=uT[:, 16:32], op=ALU.subtract),
                         reads=[b_fix, b_uT], writes=[b_pooled])
                    for Q in range(4):
                        m_ = Q % 2
                        P.op("pe", lambda e: e.matmul(out=psM[m_][:, :], lhsT=wpl[:, g, :], rhs=pooled[:, Q * 512:(Q + 1) * 512], start=True, stop=True),
                             reads=[b_wpl, b_pooled], writes=[b_psM[m_]])
                        P.op("dve", lambda e: e.scalar_tensor_tensor(out=aT[:, g, Q * 512:(Q + 1) * 512], in0=psM[m_][:, :], scalar=psc[:, g:g + 1],
                                                                     in1=sgT[:, Q * 512:(Q + 1) * 512], op0=ALU.mult, op1=ALU.mult),
                             reads=[b_psM[m_], b_psc, b_sgT], writes=[b_aT[g * 4 + Q]])
                if stage == 3:
                    dump("aT", aT[:, 0:4, :], [128, 4, S], b_aT[0:16], BF16)
                    P.end()
                    return nc
                P.flush()

        with ExitStack() as L:
            NSC = 3
            sc = [sb("sc%d" % i, [128, S], F32, L) for i in range(NSC)]
            Rb = [sb("Rb%d" % i, [128, 512], BF16, L) for i in range(3)]
            Dm = [sb("Dm%d" % i, [128, 8, 128], BF16, L) for i in range(2)]
            maskm = [sb("maskm%d" % i, [128, S], BF16, L) for i in range(2)]
            mb = [sb("mb0", [128, 16, 512], BF16, L), sb("mb1", [128, 12, 512], BF16, L)]
            Pt = [sb("Pt%d" % i, [128, 512], BF16, L) for i in range(3)]
            o_sb = sb("o_sb", [128, 8, 4, 65], F32, L)
            att = sb("att", [128, 4, 8, 64], F32, L)
            attg = sb("attg", [128, 4, 512], BF16, L)
            smt = [sb("sm%d" % i, [128, 8], F32, L) for i in range(2)]
            hkt = [sb("hk%d" % i, [128, KBIS + 1], F32, L) for i in range(2)]
            rden = sb("rden", [128, 8, 4], F32, L)
            psL = [ps("psL%d" % i, [128, 512], F32, L) for i in range(2)]
            psC = [ps("psC%d" % i, [128, 512], F32, L) for i in range(2)]
            psS = [ps("psS%d" % i, [128, 512], F32, L) for i in range(2)]
            psO = ps("psO", [128, 512], F32, L)
            psX = ps("psX", [128, 8, 128], BF16, L)
            b_sc, b_Rb, b_Dm, b_mask, b_mb, b_Pt = P.bufs("sc", NSC), P.bufs("Rb", 3), P.bufs("Dm", 2), P.bufs("maskm", 2), P.bufs("mb", 2), P.bufs("Pt", 3)
            b_osb = P.bufs("osb", 8)
            b_att, b_attg, b_rden = [P.buf(n) for n in "att attg rden".split()]
            b_sm, b_hk = P.bufs("sm", 2), P.bufs("hk", 2)
            b_psL, b_psC, b_psS = P.bufs("psL", 2), P.bufs("psC", 2), P.bufs("psS", 2)
            b_psO, b_psX = P.buf("psO"), P.buf("psX")
            psO3 = psO[:, 0:260].rearrange("p (a b) -> p a b", a=4)
            ctr = {"L": 0, "C": 0, "S": 0}
            order = [0, 1, 2, 3, 4, 5, 6, 7, 12, 13, 14, 15, 8, 9, 10, 11]
            posof = {t_: k_ for k_, t_ in enumerate(order)}
            MBOF = {0: 0, 1: 1, 3: 0, 2: 1}

            def front(i):
                N = 128 * (i + 1)
                d_, scb, b_scb = posof[i] % 2, sc[posof[i] % NSC], b_sc[posof[i] % NSC]
                for h in range(8):
                    P.op("pool", lambda e: e.tensor_scalar(out=Dm[d_][:, h, :], in0=ident[:], scalar1=wi_all[:, i, h:h + 1], scalar2=0.0,
                                                           op0=ALU.mult, op1=ALU.add), reads=[b_ident], writes=[b_Dm[d_]])
                for c in range((N + 511) // 512):
                    w_ = min(512, N - 512 * c)
                    c_ = ctr["C"] % 2
                    ctr["C"] += 1
                    pend = None
                    for h in range(8):
                        par, h4 = h % 2, h // 2
                        l_ = ctr["L"] % 2
                        r_ = ctr["L"] % 3
                        ctr["L"] += 1
                        P.op("pe", lambda e: e.matmul(out=psL[l_][:, 0:w_], lhsT=QIT[:, h4, i * 128:(i + 1) * 128],
                                                      rhs=KIT[:, par, c * 512:c * 512 + w_], start=True, stop=True),
                             writes=[b_psL[l_]])
                        P.op("act", lambda e: e.activation(out=Rb[r_][:, 0:w_], in_=psL[l_][:, 0:w_], func=AF.Relu),
                             reads=[b_psL[l_]], writes=[b_Rb[r_]])
                        if pend is not None:
                            ph, pr = pend
                            P.op("pe", lambda e: e.matmul(out=psC[c_][:, 0:w_], lhsT=Dm[d_][:, ph, :], rhs=Rb[pr][:, 0:w_], start=(ph == 0), stop=False),
                                 reads=[b_Dm[d_], b_Rb[pr]], writes=[b_psC[c_]], signal=True)
                        pend = (h, r_)
                    ph, pr = pend
                    P.op("pe", lambda e: e.matmul(out=psC[c_][:, 0:w_], lhsT=Dm[d_][:, ph, :], rhs=Rb[pr][:, 0:w_], start=False, stop=True),
                         reads=[b_Dm[d_], b_Rb[pr]], writes=[b_psC[c_]], signal=True)
                    P.op("act", lambda e: e.activation(out=scb[:, c * 512:c * 512 + w_], in_=psC[c_][:, 0:w_], func=AF.Copy),
                         reads=[b_psC[c_]], writes=[b_scb])

            def bisect(i):
                N = 128 * (i + 1)
                scb, b_scb = sc[posof[i] % NSC], b_sc[posof[i] % NSC]
                sm, bsm, hk, bhk = smt[posof[i] % 2], b_sm[posof[i] % 2], hkt[posof[i] % 2], b_hk[posof[i] % 2]
                RMAX, RMIN, R0, TT, CNT, DD, THR = [sm[:, k:k + 1] for k in range(7)]
                if i >= 2:
                    P.op("dve", lambda e: e.tensor_reduce(out=RMAX, in_=scb[:, 0:N], axis=AX.X, op=ALU.max, apply_absolute_value=True),
                         reads=[b_scb], writes=[bsm])
                P.op("dve", lambda e: e.tensor_tensor(out=scb[:, N - 128:N], in0=scb[:, N - 128:N], in1=causal[:], op=ALU.add),
                     reads=[b_scb, b_causal], writes=[b_scb])
                if i >= 2:
                    P.op("dve", lambda e: e.tensor_scalar(out=hk[:], in0=pow2[:], scalar1=RMAX, scalar2=2.0, op0=ALU.mult, op1=ALU.mult),
                         reads=[bsm, b_pow2], writes=[bhk])
                    P.op("dve", lambda e: e.tensor_tensor(out=TT, in0=hk[:, 0:1], in1=RMAX, op=ALU.subtract), reads=[bhk, bsm], writes=[bsm])
                    for k in range(KBIS):
                        P.op("dve", lambda e: e.tensor_scalar(out=maskm[posof[i] % 2][:, 0:N], in0=scb[:, 0:N], scalar1=TT, scalar2=None, op0=ALU.is_ge,
                                                              op1=ALU.add, accum_out=CNT), reads=[b_scb, bsm], writes=[b_mask[posof[i] % 2], bsm])
                        P.op("dve", lambda e: e.tensor_scalar(out=DD, in0=CNT, scalar1=255.5, scalar2=0.5, op0=ALU.is_ge, op1=ALU.subtract),
                             reads=[bsm], writes=[bsm])
                        P.op("dve", lambda e: e.scalar_tensor_tensor(out=TT, in0=DD, scalar=hk[:, k:k + 1], in1=TT, op0=ALU.mult, op1=ALU.add),
                             reads=[bsm, bhk], writes=[bsm])
                    P.op("dve", lambda e: e.tensor_tensor(out=THR, in0=TT, in1=hk[:, KBIS:KBIS + 1], op=ALU.subtract), reads=[bsm, bhk], writes=[bsm])
                    thr_ap, thr_b = THR, bsm
                else:
                    thr_ap, thr_b = thr0[:, 0:1], b_thr0
                mk, b_mk = maskm[posof[i] % 2], b_mask[posof[i] % 2]
                P.op("dve", lambda e: e.tensor_scalar(out=mk[:, 0:N], in0=scb[:, 0:N], scalar1=thr_ap, scalar2=1.0, op0=ALU.is_ge, op1=ALU.subtract),
                     reads=[b_scb, thr_b], writes=[b_mk])

            def trans(i):
                qc, il = i // 4, i % 4
                mbq, b_mbq = mb[MBOF[qc]], b_mb[MBOF[qc]]
                mk, b_mk = maskm[posof[i] % 2], b_mask[posof[i] % 2]
                for j0 in range(0, i + 1, 8):
                    n_ = min(8, i + 1 - j0)
                    for jj in range(n_):
                        j = j0 + jj
                        P.op("pe", lambda e: e.transpose(out=psX[:, jj, :], in_=mk[:, j * 128:(j + 1) * 128], identity=ident[:]),
                             reads=[b_mk, b_ident], writes=[b_psX], signal=(jj == n_ - 1))
                    P.op("act", lambda e: e.activation(out=mbq[:, j0:j0 + n_, il * 128:(il + 1) * 128], in_=psX[:, 0:n_, :], func=AF.Copy, scale=NEGM),
                         reads=[b_psX], writes=[b_mbq])

            def attn_head(qc, h):
                mbq, b_mbq = mb[MBOF[qc]], b_mb[MBOF[qc]]
                nj = 4 * (qc + 1)
                par, h4, g = h % 2, h // 2, h // 4
                state = {"first": True}

                def pv(j, p_):
                    ils = [il for il in range(4) if j <= 4 * qc + il]
                    for il in ils:
                        last = (j == nj - 1) and (il == ils[-1])
                        P.op("pe", lambda e: e.matmul(out=psO3[:, il, :], lhsT=Pt[p_][:, il * 128:(il + 1) * 128], rhs=Vx[:, j, g, :],
                                                      start=state["first"], stop=last, skip_group_check=True),
                             reads=[b_Pt[p_], b_Vx], writes=[b_psO], signal=(il == ils[-1]))
                        state["first"] = False

                prev = None
                for j in range(nj):
                    s_ = ctr["S"] % 2
                    p_ = ctr["S"] % 3
                    ctr["S"] += 1
                    t0 = 128 * max(0, j - 4 * qc)
                    P.op("pe", lambda e: e.matmul(out=psS[s_][:, t0:512], lhsT=KT[:, g * 2 + par, j * 128:(j + 1) * 128],
                                                  rhs=QT[:, h4, qc * 512 + t0:(qc + 1) * 512], start=True, stop=False),
                         writes=[b_psS[s_]], signal=False)
                    P.op("pe", lambda e: e.matmul(out=psS[s_][:, t0:512], lhsT=ident[:], rhs=mbq[:, j, t0:512], start=False, stop=True),
                         reads=[b_mbq, b_ident], writes=[b_psS[s_]])
                    P.op("act", lambda e: e.activation(out=Pt[p_][:, t0:512], in_=psS[s_][:, t0:512], func=AF.Exp, scale=0.125),
                         reads=[b_psS[s_]], writes=[b_Pt[p_]])
                    if prev is not None:
                        pv(*prev)
                    prev = (j, p_)
                pv(*prev)
                P.op("act", lambda e: e.activation(out=o_sb[:, h, :, :], in_=psO3, func=AF.Copy), reads=[b_psO], writes=[b_osb[h]])

            def attn_final(qc):
                P.op("dve", lambda e: e.reciprocal(out=rden[:], in_=o_sb[:, :, :, 64]), reads=b_osb, writes=[b_rden])
                P.op("dve", lambda e: e.tensor_tensor(out=att[:].rearrange("p a h d -> p h a d"), in0=o_sb[:, :, :, 0:64],
                                                      in1=rden[:].unsqueeze(3).to_broadcast([128, 8, 4, 64]), op=ALU.mult),
                     reads=b_osb + [b_rden], writes=[b_att])
                P.op("pool", lambda e: e.tensor_tensor(out=attg[:], in0=att[:].rearrange("p a h d -> p a (h d)"), in1=sga[:, 4 * qc:4 * qc + 4, :], op=ALU.mult),
                     reads=[b_att], writes=[b_attg])
                for half in range(2):
                    for bl in range(2):
                        blk = half * 2 + bl
                        for il in range(4):
                            P.op("pe", lambda e: e.transpose(out=psX[:, bl * 4 + il, :], in_=attg[:, il, blk * 128:(blk + 1) * 128], identity=ident[:]),
                                 reads=[b_attg, b_ident], writes=[b_psX], signal=(bl == 1 and il == 3))
                    for bl in range(2):
                        blk = half * 2 + bl
                        P.op("act", lambda e: e.activation(out=aT[:, 4 + blk, qc * 512:(qc + 1) * 512],
                                                           in_=psX[:, bl * 4:bl * 4 + 4, :].rearrange("p a b -> p (a b)"), func=AF.Copy),
                             reads=[b_psX], writes=[b_aT[16 + blk * 4 + qc]])

            queue = []
            mb_owner = {0: None, 1: None}

            def run_unit(u):
                kind, qc_, h_ = u
                attn_head(qc_, h_) if kind == "h" else attn_final(qc_)

            for s_ in range(NT + 3):
                if s_ < NT:
                    front(order[s_])
                if 2 <= s_ < NT + 2:
                    bisect(order[s_ - 2])
                if 3 <= s_:
                    t_ = order[s_ - 3]
                    qc_t = t_ // 4
                    if t_ % 4 == 0:
                        prev_owner = mb_owner[MBOF[qc_t]]
                        if prev_owner is not None:
                            while any(u[1] == prev_owner for u in queue):
                                run_unit(queue.pop(0))
                        mb_owner[MBOF[qc_t]] = qc_t
                    trans(t_)
                    if t_ % 4 == 3:
                        queue += [("h", qc_t, h) for h in range(8)] + [("f", qc_t, 0)]
                for _ in range(2):
                    if queue:
                        run_unit(queue.pop(0))
            while queue:
                run_unit(queue.pop(0))
            if stage == 4:
                dump("aT", aT[:], [128, 8, S], b_aT, BF16)
                P.end()
                return nc
            P.flush()

        with ExitStack() as L:
            wst = [sb("wst%d" % i, [128, 1024], F32, L) for i in range(4)]
            wo = sb("wo", [128, 8, 1024], BF16, L)
            xr = [sb("xr%d" % i, [128, 1024], F32, L) for i in range(4)]
            ot = [sb("ot%d" % i, [128, 1024], F32, L) for i in range(3)]
            psY = [ps("psY%d" % i, [128, 512], F32, L) for i in range(4)]
            b_wst, b_xr, b_ot, b_psY = P.bufs("wst", 4), P.bufs("xr", 4), P.bufs("ot", 3), P.bufs("psY", 4)
            b_wo = P.bufs("wo", 8)
            wout_v = wout_d.rearrange("(ec p) d -> ec p d", p=128)
            for ec in range(8):
                k = ec % 4
                P.dma("sp", wst[k][:], wout_v[ec], writes=[b_wst[k]])
                P.op("pool" if ec % 2 else "dve", lambda e: e.tensor_tensor(out=wo[:, ec, :], in0=wst[k][:], in1=gate_b[:], op=ALU.mult),
                     reads=[b_wst[k], b_gate], writes=[b_wo[ec]])
            xv = x_d.rearrange("(n p) d -> n p d", p=128)
            ov = out_d.rearrange("(n p) d -> n p d", p=128)
            for i in range(NT):
                kx = i % 4
                ko = i % 3
                P.dma("sp", xr[kx][:], xv[i], writes=[b_xr[kx]])
                for hf in range(2):
                    y_ = (i % 2) * 2 + hf
                    for ec in range(8):
                        P.op("pe", lambda e: e.matmul(out=psY[y_][:, :], lhsT=aT[:, ec, i * 128:(i + 1) * 128], rhs=wo[:, ec, hf * 512:(hf + 1) * 512],
                                                      start=(ec == 0), stop=(ec == 7)), reads=[b_wo[ec]], writes=[b_psY[y_]], signal=(ec == 7))
                    P.op("dve", lambda e: e.tensor_tensor(out=ot[ko][:, hf * 512:(hf + 1) * 512], in0=psY[y_][:, :], in1=xr[kx][:, hf * 512:(hf + 1) * 512], op=ALU.add),
                         reads=[b_psY[y_], b_xr[kx]], writes=[b_ot[ko]])
                P.dma("act", ov[i], ot[ko][:], reads=[b_ot[ko]], out_final=True)
            P.end()
    return nc


def make_in_maps(x, c, norm_w, w_ada, b_ada, w_in, q_norm_w, k_norm_w, w_pool, pool_scale, w_out, cores=range(8)):
    f = lambda a: np.ascontiguousarray(np.asarray(a, dtype=np.float32))
    hc = host_consts()
    shared = {
        "nw_col": f(np.asarray(norm_w)[0].reshape(8, 128).T),
        "bada_col": f(np.asarray(b_ada)[0, :2048].reshape(16, 128).T),
        "bgate_row": f(np.asarray(b_ada)[0, 2048:].reshape(1, 1024)),
        "w_ada": f(np.asarray(w_ada)[0]),
        "w_in": f(np.asarray(w_in)[0]),
        "w_out": f(np.asarray(w_out)[0]),
        "w_pool": f(np.asarray(w_pool)[0].transpose(1, 0, 2)),
        "pscale_col": f(np.asarray(pool_scale)[0].reshape(4, 128).T),
        "qnw_b": f(np.tile(np.asarray(q_norm_w)[0][None, :], (128, 1))),
        "knw_b": f(np.tile(np.asarray(k_norm_w)[0][None, :], (128, 1))),
    }
    shared.update(hc)
    maps = []
    for b in cores:
        m = dict(shared)
        m["x"] = f(np.asarray(x)[b])
        m["c_col"] = f(np.asarray(c)[b].reshape(8, 128).T)
        maps.append(m)
    return maps


_NC_CACHE = {}


def kernel(x, c, norm_w, w_ada, b_ada, w_in, q_norm_w, k_norm_w, w_pool, pool_scale, w_out):
    if "nc" not in _NC_CACHE:
        _NC_CACHE["nc"] = build()
    nc = _NC_CACHE["nc"]
    maps = make_in_maps(x, c, norm_w, w_ada, b_ada, w_in, q_norm_w, k_norm_w, w_pool, pool_scale, w_out)
    res = run_bass_kernel_spmd(nc, maps, core_ids=list(range(8)))
    return np.stack([np.asarray(r["out"], dtype=np.float32) for r in res.results], axis=0)
```

```python
import numpy as np
import ml_dtypes
from contextlib import ExitStack
import concourse.bass as bass
import concourse.mybir as mybir
from concourse.bass_utils import run_bass_kernel_spmd

F32 = mybir.dt.float32
BF16 = mybir.dt.bfloat16
ALU = mybir.AluOpType
AF = mybir.ActivationFunctionType
AX = mybir.AxisListType

N_DMA_SEMS = 20


class Buf:
    __slots__ = ("name", "w", "r")

    def __init__(self, name):
        self.name = name
        self.w = None
        self.r = []


class _Rec:
    def __init__(self):
        self.call = None

    def __getattr__(self, name):
        def f(*a, **k):
            self.call = (name, a, k)
            return self
        return f


class Prog:
    ENGS = ("pe", "act", "dve", "pool", "sp")

    def __init__(self, nc):
        self.nc = nc
        self.stack = ExitStack()
        self.sems = {}
        self.cnt = {}
        self.floor = {}
        self.seen = {e: {} for e in self.ENGS}
        self.ops = {e: [] for e in self.ENGS}
        self.pending = {e: [] for e in self.ENGS}
        self.dma_rr = 0
        self.dma_last = {}
        self.out_toks = []
        self.started = False

    def begin(self):
        nc = self.nc
        for e in self.ENGS:
            self.sems[e] = self.stack.enter_context(nc.semaphore("sem_" + e))
            self.cnt[e] = 0
            self.floor[e] = 0
        for i in range(N_DMA_SEMS):
            k = "dma%d" % i
            self.sems[k] = self.stack.enter_context(nc.semaphore("sem_" + k))
            self.cnt[k] = 0
        self.started = True

    def buf(self, name="b"):
        return Buf(name)

    def bufs(self, name, n):
        return [Buf("%s%d" % (name, i)) for i in range(n)]

    def _engobj(self, blk_engine):
        return blk_engine

    def _collect(self, eng, reads, writes):
        waits = {}

        def need(tok):
            if tok is None:
                return
            key, c = tok
            if key in self.floor and c <= self.floor[key]:
                return
            if self.seen[eng].get(key, 0) >= c:
                return
            if waits.get(key, 0) < c:
                waits[key] = c

        for b in reads:
            if b.w is not None:
                if b.w[0] == eng and eng == "pe":
                    continue
                need(b.w)
        for b in writes:
            if b.w is not None and (b.w[0] != eng or eng != "pe"):
                need(b.w)
            for t in b.r:
                if t[0] != eng or eng != "pe":
                    need(t)
        for k, c in waits.items():
            self.seen[eng][k] = c
        return list(waits.items())

    def _check_pending(self, eng, reads, writes):
        for e2 in self.ENGS:
            if e2 == eng:
                continue
            for (r2, w2) in self.pending[e2]:
                for b in reads:
                    if any(b is x for x in w2):
                        raise RuntimeError("dep on unsignaled op: %s" % b.name)
                for b in writes:
                    if any(b is x for x in w2) or any(b is x for x in r2):
                        raise RuntimeError("dep on unsignaled op: %s" % b.name)

    def op(self, eng, fn, reads=(), writes=(), signal=True):
        reads = list(reads)
        writes = list(writes)
        self._check_pending(eng, reads, writes)
        waits = self._collect(eng, reads, writes)
        rec = _Rec()
        fn(rec)
        name_, a_, k_ = rec.call
        fn = (lambda e, name_=name_, a_=a_, k_=k_: getattr(e, name_)(*a_, **k_))
        self.ops[eng].append((waits, fn, self.sems[eng] if signal else None, 1))
        if signal:
            self.cnt[eng] += 1
            tok = (eng, self.cnt[eng])
            for (r2, w2) in self.pending[eng]:
                for b in w2:
                    b.w = tok
                    b.r = []
                for b in r2:
                    b.r.append(tok)
            self.pending[eng] = []
            for b in writes:
                b.w = tok
                b.r = []
            for b in reads:
                b.r.append(tok)
            return tok
        else:
            self.pending[eng].append((reads, writes))
            return None

    def dma(self, q, out, in_, reads=(), writes=(), out_final=False, **kw):
        reads = list(reads)
        writes = list(writes)
        self._check_pending(q, reads, writes)
        if q == "pool":
            key = "swdma%d" % len([k for k in self.sems if k.startswith("swdma")])
            self.sems[key] = self.stack.enter_context(self.nc.semaphore("sem_" + key))
            self.cnt[key] = 0
        else:
            key = "dma%d" % self.dma_rr
            self.dma_rr = (self.dma_rr + 1) % N_DMA_SEMS
        waits = self._collect(q, reads, writes)
        prev = self.dma_last.get(key)
        if prev is not None and self.seen[q].get(key, 0) < prev[1]:
            waits.append((key, prev[1]))
            self.seen[q][key] = prev[1]
        self.cnt[key] += 16
        tok = (key, self.cnt[key])
        self.dma_last[key] = tok
        fn = (lambda e, o=out, i=in_, kw=kw: e.dma_start(out=o, in_=i, **kw))
        self.ops[q].append((waits, fn, self.sems[key], 16))
        for b in writes:
            b.w = tok
            b.r = []
        for b in reads:
            b.r.append(tok)
        if out_final:
            self.out_toks.append(tok)
        return tok

    def flush(self, final=False):
        nc = self.nc
        for e in self.ENGS:
            if self.pending[e]:
                raise RuntimeError("pending unsignaled ops on %s at flush" % e)
        if final:
            ws = {}
            for (k, c) in self.out_toks:
                ws[k] = max(ws.get(k, 0), c)
            self.ops["sp"].append((list(ws.items()), None, None, 0))
        sems = self.sems

        def body(ops):
            def run(e):
                for (waits, fn, sem, inc) in ops:
                    for (k, c) in waits:
                        e.wait_ge(sems[k], c)
                    if fn is None:
                        continue
                    ins = fn(e)
                    if sem is not None:
                        ins.then_inc(sem, inc)
            return run

        with nc.Block() as blk:
            m = {"pe": blk.tensor, "act": blk.scalar, "dve": blk.vector, "pool": blk.gpsimd, "sp": blk.sync}
            for e in self.ENGS:
                if self.ops[e]:
                    m[e](body(self.ops[e]))
        for e in self.ENGS:
            self.ops[e] = []
            self.floor[e] = self.cnt[e]

    def end(self):
        self.flush(final=True)
        self.stack.close()


S = 2048
D = 1024
NT = 16
EPS = 1e-6
KBIS = 10
C_U, C_GP, C_Q, C_K, C_V, C_GA, C_QI, C_KI, C_WI = 0, 512, 1024, 1536, 1664, 1792, 2304, 2816, 2880
NEGM = 30000.0


def host_consts():
    half = 32
    freqs = (10000.0 ** (-np.arange(half, dtype=np.float32) / half)).astype(np.float32)
    pos = np.arange(S, dtype=np.float32)
    ang = pos[:, None] * freqs[None, :]
    cos = np.cos(ang).astype(np.float32)
    sin = np.sin(ang).astype(np.float32)
    cos2 = np.concatenate([cos, cos], axis=1)
    sin2 = np.concatenate([-sin, sin], axis=1)
    cs = np.stack([cos2, sin2], axis=1)
    cs = cs.reshape(NT, 128, 2, 64).transpose(1, 0, 2, 3)
    t = np.arange(128)
    causal = np.where(t[None, :] <= t[:, None], 0.0, -1e30).astype(np.float32)
    pinv = np.zeros((128, 4, 16), np.float32)
    for g, w in enumerate((2, 4, 8, 16)):
        pinv[:, g, :] = 1.0 / np.minimum(np.arange(1, 17), w)
    pow2 = np.tile((2.0 ** -(np.arange(KBIS + 1) + 1.0)).astype(np.float32)[None, :], (128, 1))
    return {
        "cs": np.ascontiguousarray(cs),
        "ident": np.eye(128).astype(ml_dtypes.bfloat16),
        "identf": np.eye(128).astype(np.float32),
        "causal": causal,
        "pinv": pinv,
        "pow2": pow2,
    }


def build(stage=99):
    nc = bass.Bass("TRN2", target_bir_lowering=False)
    dt = nc.dram_tensor

    def din(name, shape, dtype=F32):
        return dt(name, list(shape), dtype, kind="ExternalInput").ap()

    x_d = din("x", [S, D])
    ccol_d = din("c_col", [128, 8])
    nw_d = din("nw_col", [128, 8])
    bada_d = din("bada_col", [128, 16])
    bgate_d = din("bgate_row", [1, 1024])
    wada_d = din("w_ada", [D, 3 * D])
    win_d = din("w_in", [D, 2888])
    wout_d = din("w_out", [D, D])
    wpool_d = din("w_pool", [128, 4, 128])
    pscale_d = din("pscale_col", [128, 4])
    qnw_d = din("qnw_b", [128, 64])
    knw_d = din("knw_b", [128, 64])
    cs_d = din("cs", [128, NT, 2, 64])
    ident_d = din("ident", [128, 128], BF16)
    identf_d = din("identf", [128, 128])
    causal_d = din("causal", [128, 128])
    pinv_d = din("pinv", [128, 4, 16])
    pow2_d = din("pow2", [128, KBIS + 1])
    out_d = dt("out", [S, D], F32, kind="ExternalOutput").ap()
    dbg_d = {}

    P = Prog(nc)
    G = ExitStack()

    def sb(name, shape, dtype=F32, stack=None):
        return (stack or G).enter_context(nc.sbuf_tensor("s_" + name, list(shape), dtype))

    def ps(name, shape, dtype=F32, stack=None):
        return (stack or G).enter_context(nc.psum_tensor("p_" + name, list(shape), dtype))

    def dump(name, ap, shape, bufs, dtype=F32):
        d = dt("dbg_" + name, list(shape), dtype, kind="ExternalOutput").ap()
        P.dma("sp", d, ap, reads=bufs, out_final=True)

    with G:
        P.begin()
        ident = sb("ident", [128, 128], BF16)
        causal = sb("causal", [128, 128])
        pow2 = sb("pow2", [128, KBIS + 1])
        GT = sb("GT", [128, 8])
        shT = sb("shT", [128, 8])
        gate_b = sb("gate_b", [128, 1024])
        aT = sb("aT", [128, 8, S], BF16)
        QT = sb("QT", [128, 4, S], BF16)
        KT = sb("KT", [128, 4, S], BF16)
        QIT = sb("QIT", [128, 4, S], BF16)
        KIT = sb("KIT", [128, 2, S], BF16)
        Vx = sb("Vx", [128, NT, 2, 65], BF16)
        sga = sb("sga", [128, NT, 512], BF16)
        wi_all = sb("wi_all", [128, NT, 8])
        thr0 = sb("thr0", [128, 1])
        b_ident, b_identf, b_causal, b_pow2 = P.buf("ident"), P.buf("identf"), P.buf("causal"), P.buf("pow2")
        b_GT, b_shT, b_gate = P.buf("GT"), P.buf("shT"), P.buf("gate")
        b_aT = P.bufs("aT", 8 * 4)
        b_Vx = P.buf("Vx")
        P.dma("sp", ident[:], ident_d, writes=[b_ident])
        P.dma("sp", causal[:], causal_d, writes=[b_causal])
        P.dma("sp", pow2[:], pow2_d, writes=[b_pow2])
        b_thr0 = P.buf("thr0")
        P.op("pool", lambda e: e.memset(thr0[:], -1e29), writes=[b_thr0])
        P.op("pool", lambda e: e.memset(Vx[:], 1.0), writes=[b_Vx])

        with ExitStack() as L:
            wa = [sb("wa%d" % i, [128, 8, 512], F32, L) for i in range(3)]
            b_wa = P.bufs("wa", 3)
            ccol = sb("ccol", [128, 8], F32, L)
            sccol = sb("sccol", [128, 8], F32, L)
            nwcol = sb("nwcol", [128, 8], F32, L)
            badac = sb("badac", [128, 16], F32, L)
            bgate = sb("bgate", [1, 1024], F32, L)
            modT = sb("modT", [128, 16], F32, L)
            grow = sb("grow", [1, 1024], F32, L)
            ones_r = sb("ones_r", [1, 128], F32, L)
            psT = ps("psT", [128, 512], F32, L)
            psG = [ps("psG%d" % i, [128, 512], F32, L) for i in range(2)]
            b_cc, b_scc, b_nw, b_bada, b_bg, b_modT, b_grow, b_ones, b_psT = [P.buf(n) for n in
                "cc scc nw bada bg modT grow ones psT".split()]
            b_psG = P.bufs("psG", 2)
            P.dma("sp", ccol[:], ccol_d, writes=[b_cc])
            P.dma("sp", nwcol[:], nw_d, writes=[b_nw])
            P.dma("sp", badac[:], bada_d, writes=[b_bada])
            P.dma("sp", bgate[:], bgate_d, writes=[b_bg])
            P.op("pool", lambda e: e.memset(ones_r[:], 1.0), writes=[b_ones])
            P.op("act", lambda e: e.activation(out=sccol[:], in_=ccol[:], func=AF.Silu), reads=[b_cc], writes=[b_scc])
            wada_v = wada_d.rearrange("(kc p) e -> p kc e", p=128)
            for cb in range(6):
                wb_, bb_ = wa[cb % 3], b_wa[cb % 3]
                P.dma("sp" if cb % 2 == 0 else "act", wb_[:], wada_v[:, :, cb * 512:(cb + 1) * 512], writes=[bb_])
                if cb < 4:
                    for jb in range(4):
                        col = cb * 4 + jb
                        for kc in range(8):
                            P.op("pe", lambda e, wb_=wb_, kc=kc, jb=jb, col=col: e.matmul(
                                out=psT[:, col:col + 1], lhsT=wb_[:, kc, jb * 128:(jb + 1) * 128],
                                rhs=sccol[:, kc:kc + 1], start=(kc == 0), stop=(kc == 7)),
                                reads=[bb_, b_scc], writes=[b_psT], signal=(kc == 7))
                else:
                    hb = cb - 4
                    for kc in range(8):
                        P.op("pe", lambda e, wb_=wb_, kc=kc, hb=hb: e.matmul(
                            out=psG[hb][0:1, :], lhsT=sccol[:, kc:kc + 1], rhs=wb_[:, kc, :],
                            start=(kc == 0), stop=(kc == 7)),
                            reads=[bb_, b_scc], writes=[b_psG[hb]], signal=(kc == 7))
                    P.op("dve", lambda e, hb=hb: e.tensor_tensor(out=grow[0:1, hb * 512:(hb + 1) * 512], in0=psG[hb][0:1, :],
                                                                in1=bgate[0:1, hb * 512:(hb + 1) * 512], op=ALU.add),
                         reads=[b_psG[hb], b_bg], writes=[b_grow])
            P.op("dve", lambda e: e.tensor_tensor(out=modT[:], in0=psT[:, 0:16], in1=badac[:], op=ALU.add),
                 reads=[b_psT, b_bada], writes=[b_modT])
            P.op("dve", lambda e: e.scalar_tensor_tensor(out=GT[:], in0=modT[:, 8:16], scalar=1.0, in1=nwcol[:],
                                                         op0=ALU.add, op1=ALU.mult), reads=[b_modT, b_nw], writes=[b_GT])
            P.op("dve", lambda e: e.tensor_copy(out=shT[:], in_=modT[:, 0:8]), reads=[b_modT], writes=[b_shT])
            for hb in range(2):
                P.op("pe", lambda e, hb=hb: e.matmul(out=psG[hb][:, :], lhsT=ones_r[0:1, :], rhs=grow[0:1, hb * 512:(hb + 1) * 512],
                                                     start=True, stop=True), reads=[b_ones, b_grow], writes=[b_psG[hb]])
                P.op("act", lambda e, hb=hb: e.activation(out=gate_b[:, hb * 512:(hb + 1) * 512], in_=psG[hb][:, :], func=AF.Copy),
                     reads=[b_psG[hb]], writes=[b_gate])
            if stage == 0:
                dump("GT", GT[:], [128, 8], [b_GT])
                dump("shT", shT[:], [128, 8], [b_shT])
                dump("gate_b", gate_b[:], [128, 1024], [b_gate])
                P.end()
                return nc
            P.flush()

        with ExitStack() as H:
            hT = sb("hT", [128, 8, S], BF16, H)
            b_hT = P.bufs("hT", NT)
            with ExitStack() as L:
                identf = sb("identf", [128, 128], F32, L)
                P.dma("sp", identf[:], identf_d, writes=[b_identf])
                xt = [sb("xt%d" % i, [128, 1024], F32, L) for i in range(4)]
                xn = [sb("xn%d" % i, [128, 1024], F32, L) for i in range(2)]
                junk = sb("junk1", [128, 1024], BF16, L)
                tmpf = [sb("tmpf%d" % i, [128, 8, 128], F32, L) for i in range(2)]
                ss = sb("ss", [128, NT], F32, L)
                rstd = sb("rstd", [128, NT], F32, L)
                psX = [ps("psX%d" % i, [128, 8, 128], F32, L) for i in range(2)]
                b_xt, b_xn, b_tmpf, b_psX = P.bufs("xt", 4), P.bufs("xn", 2), P.bufs("tmpf", 2), P.bufs("psX", 2)
                b_junk = P.buf("junk")
                b_ss = P.bufs("ss", NT)
                xv = x_d.rearrange("(n p) d -> n p d", p=128)
                def a1_stage0(i):
                    k4 = i % 4
                    P.dma("sp", xt[k4][:], xv[i], writes=[b_xt[k4]])
                    P.op("act", lambda e: e.activation(out=junk[:], in_=xt[k4][:], func=AF.Square, accum_out=ss[:, i:i + 1]),
                         reads=[b_xt[k4]], writes=[b_junk, b_ss[i]])

                def a1_stage1(i):
                    k = i % 2
                    k4 = i % 4
                    P.op("dve", lambda e: e.tensor_scalar(out=ss[:, i:i + 1], in0=ss[:, i:i + 1], scalar1=1.0 / D, scalar2=EPS,
                                                          op0=ALU.mult, op1=ALU.add), reads=[b_ss[i]], writes=[b_ss[i]])
                    P.op("act", lambda e: e.activation(out=ss[:, i:i + 1], in_=ss[:, i:i + 1], func=AF.Sqrt),
                         reads=[b_ss[i]], writes=[b_ss[i]])
                    P.op("dve", lambda e: e.reciprocal(out=rstd[:, i:i + 1], in_=ss[:, i:i + 1]), reads=[b_ss[i]], writes=[b_ss[i]])
                    P.op("act", lambda e: e.activation(out=xn[k][:], in_=xt[k4][:], func=AF.Copy, scale=rstd[:, i:i + 1]),
                         reads=[b_xt[k4], b_ss[i]], writes=[b_xn[k]])

                def a1_stage2(i):
                    k = i % 2
                    for kc in range(8):
                        P.op("pe", lambda e: e.transpose(out=psX[k][:, kc, :], in_=xn[k][:, kc * 128:(kc + 1) * 128], identity=identf[:]),
                             reads=[b_xn[k], b_identf], writes=[b_psX[k]], signal=(kc == 7))
                    P.op("dve", lambda e: e.tensor_tensor(out=tmpf[k][:], in0=psX[k][:], in1=GT[:].unsqueeze(2).to_broadcast([128, 8, 128]),
                                                          op=ALU.mult), reads=[b_psX[k], b_GT], writes=[b_tmpf[k]])
                    P.op("pool", lambda e: e.tensor_tensor(out=hT[:, :, i * 128:(i + 1) * 128], in0=tmpf[k][:],
                                                           in1=shT[:].unsqueeze(2).to_broadcast([128, 8, 128]), op=ALU.add),
                         reads=[b_tmpf[k], b_shT], writes=[b_hT[i]])

                for i_ in range(NT + 2):
                    if i_ < NT:
                        a1_stage0(i_)
                    if 1 <= i_ < NT + 1:
                        a1_stage1(i_ - 1)
                    if i_ >= 2:
                        a1_stage2(i_ - 2)
                if stage == 1:
                    dump("hT", hT[:], [128, 8, S], b_hT, BF16)
                    P.end()
                    return nc
                P.flush()

            with ExitStack() as L:
                NA = 2888 - C_Q
                wA = sb("wA", [128, 8, NA], BF16, L)
                b_wA = P.bufs("wA", 4)
                win_v = win_d.rearrange("(kc p) e -> p kc e", p=128)
                bounds = [0, 512, 768, 1280, NA]
                for j in range(4):
                    P.dma("pool", wA[:, :, bounds[j]:bounds[j + 1]], win_v[:, :, C_Q + bounds[j]:C_Q + bounds[j + 1]], writes=[b_wA[j]])
                cs = sb("cs", [128, NT, 2, 64], F32, L)
                qnw = sb("qnw", [128, 64], F32, L)
                knw = sb("knw", [128, 64], F32, L)
                b_cs, b_qnw, b_knw = P.buf("cs"), P.buf("qnw"), P.buf("knw")
                P.dma("sp", cs[:], cs_d, writes=[b_cs])
                P.dma("sp", qnw[:], qnw_d, writes=[b_qnw])
                P.dma("sp", knw[:], knw_d, writes=[b_knw])
                psQ = ps("psQ", [128, 512], F32, L)
                psKV = ps("psKV", [128, 512], F32, L)
                psGA = ps("psGA", [128, 512], F32, L)
                psQI = ps("psQI", [128, 512], F32, L)
                psTR = [ps("psTR%d" % i, [128, 8, 128], BF16, L) for i in range(2)]
                b_psQ, b_psKV, b_psGA, b_psQI = P.buf("psQ"), P.buf("psKV"), P.buf("psGA"), P.buf("psQI")
                b_psTR = P.bufs("psTR", 2)
                q_sb_R = [sb("q_sb%d" % r_, [128, 8, 64], F32, L) for r_ in range(2)]
                qi_sb_R = [sb("qi_sb%d" % r_, [128, 8, 64], F32, L) for r_ in range(2)]
                k_sb_R = [sb("k_sb%d" % r_, [128, 2, 64], F32, L) for r_ in range(2)]
                ki_sb_R = [sb("ki_sb%d" % r_, [128, 1, 64], F32, L) for r_ in range(2)]
                sq = sb("sq", [128, 8, 64], F32, L)
                ssq_R = [sb("ssq%d" % r_, [128, 8], F32, L) for r_ in range(2)]
                kss_R = [sb("kss%d" % r_, [128, 2], F32, L) for r_ in range(2)]
                qg = sb("qg", [128, 8, 64], F32, L)
                kg = sb("kg", [128, 2, 64], F32, L)
                kt1 = sb("kt1", [128, 2, 64], F32, L)
                kt2 = sb("kt2", [128, 2, 64], F32, L)
                ksq = kt2
                it2 = sb("it2", [128, 8, 64], F32, L)
                jt2 = sb("jt2", [128, 1, 64], F32, L)
                qr_R = [sb("qr%d" % r_, [128, 8, 64], BF16, L) for r_ in range(2)]
                qir_R = [sb("qir%d" % r_, [128, 8, 64], BF16, L) for r_ in range(2)]
                krd_R = [sb("krd%d" % r_, [128, 2, 2, 128], BF16, L) for r_ in range(2)]
                kird_R = [sb("kird%d" % r_, [128, 2, 128], BF16, L) for r_ in range(2)]
                names = "q_sb qi_sb k_sb ki_sb sq ssq kss qg kg kt1 kt2 it2 jt2 qr qir krd kird".split()
                bb0 = {n: P.buf(n) for n in names}
                bb0["ksq"] = bb0["kt2"]
                RING = "q_sb qi_sb k_sb ki_sb qr qir krd kird ssq kss".split()
                bbR = [{n: P.buf(n + str(r_)) for n in RING} for r_ in range(2)]
                bb = dict(bb0)
                b_QT, b_KT, b_QIT, b_KIT, b_sga, b_wi = [P.bufs(n, NT) for n in "QT KT QIT KIT sga wi".split()]
                for r_ in range(2):
                    P.op("pool", lambda e: e.memset(krd_R[r_][:], 0.0), writes=[bbR[r_]["krd"]])
                    P.op("pool", lambda e: e.memset(kird_R[r_][:], 0.0), writes=[bbR[r_]["kird"]])

                def bc(ap, n):
                    return ap.unsqueeze(1).to_broadcast([128, n, 64])

                def rope(eng, src, nsrc, nh, i, ta, tb, na, nb, bb):
                    cosb = bc(cs[:, i, 0, :], nh)
                    P.op(eng, lambda e: e.tensor_tensor(out=tb[:, :, 0:32], in0=src[:, :, 32:64],
                                                        in1=cs[:, i, 1, 0:32].unsqueeze(1).to_broadcast([128, nh, 32]), op=ALU.mult),
                         reads=[bb[nsrc], b_cs], writes=[bb[nb]])
                    P.op(eng, lambda e: e.tensor_tensor(out=tb[:, :, 32:64], in0=src[:, :, 0:32],
                                                        in1=cs[:, i, 1, 32:64].unsqueeze(1).to_broadcast([128, nh, 32]), op=ALU.mult),
                         reads=[bb[nsrc], b_cs], writes=[bb[nb]])
                    P.op(eng, lambda e: e.tensor_tensor(out=ta[:], in0=src[:], in1=cosb, op=ALU.mult),
                         reads=[bb[nsrc], b_cs], writes=[bb[na]])

                def compute_a(i):
                    rr = i % 2
                    q_sb, qi_sb, k_sb, ki_sb, qr, qir, krd, kird = (q_sb_R[rr], qi_sb_R[rr], k_sb_R[rr], ki_sb_R[rr], qr_R[rr], qir_R[rr], krd_R[rr], kird_R[rr])
                    ssq, kss = ssq_R[rr], kss_R[rr]
                    bb = dict(bb0)
                    bb.update(bbR[rr])
                    tk = slice(i * 128, (i + 1) * 128)
                    def proj(pst, bps, c0, c1, o0, wbufs):
                        for kc in range(8):
                            P.op("pe", lambda e, kc=kc: e.matmul(out=pst[:, o0:o0 + (c1 - c0)], lhsT=hT[:, kc, tk], rhs=wA[:, kc, c0:c1],
                                                                 start=(kc == 0), stop=(kc == 7)),
                                 reads=[b_hT[i]] + wbufs, writes=[bps], signal=(kc == 7))
                    proj(psQ, b_psQ, 0, 512, 0, [b_wA[0]])
                    P.op("act", lambda e: e.activation(out=q_sb[:].rearrange("p h d -> p (h d)"), in_=psQ[:, :], func=AF.Copy),
                         reads=[b_psQ], writes=[bb["q_sb"]])
                    proj(psKV, b_psKV, 512, 768, 0, [b_wA[1]])
                    proj(psKV, b_psKV, 1792, 1864, 256, [b_wA[3]])
                    P.op("act", lambda e: e.activation(out=k_sb[:].rearrange("p h d -> p (h d)"), in_=psKV[:, 0:128], func=AF.Copy),
                         reads=[b_psKV], writes=[bb["k_sb"]])
                    P.op("act", lambda e, i=i: e.activation(out=Vx[:, i, :, 0:64], in_=psKV[:, 128:256].rearrange("p (g d) -> p g d", g=2), func=AF.Copy),
                         reads=[b_psKV], writes=[b_Vx])
                    P.op("act", lambda e: e.activation(out=ki_sb[:].rearrange("p h d -> p (h d)"), in_=psKV[:, 256:320], func=AF.Copy),
                         reads=[b_psKV], writes=[bb["ki_sb"]])
                    P.op("act", lambda e, i=i: e.activation(out=wi_all[:, i, :], in_=psKV[:, 320:328], func=AF.Copy),
                         reads=[b_psKV], writes=[b_wi[i]])
                    proj(psGA, b_psGA, 768, 1280, 0, [b_wA[2]])
                    P.op("act", lambda e, i=i: e.activation(out=sga[:, i, :], in_=psGA[:, :], func=AF.Silu),
                         reads=[b_psGA], writes=[b_sga[i]])
                    proj(psQI, b_psQI, 1280, 1792, 0, [b_wA[3]])
                    P.op("act", lambda e: e.activation(out=qi_sb[:].rearrange("p h d -> p (h d)"), in_=psQI[:, :], func=AF.Copy),
                         reads=[b_psQI], writes=[bb["qi_sb"]])
                    P.op("dve", lambda e: e.tensor_tensor(out=sq[:], in0=q_sb[:], in1=q_sb[:], op=ALU.mult), reads=[bb["q_sb"]], writes=[bb["sq"]])
                    P.op("dve", lambda e: e.tensor_reduce(out=ssq[:], in_=sq[:], axis=AX.X, op=ALU.add), reads=[bb["sq"]], writes=[bb["ssq"]])
                    P.op("dve", lambda e: e.tensor_scalar(out=ssq[:], in0=ssq[:], scalar1=1.0 / 64, scalar2=EPS, op0=ALU.mult, op1=ALU.add),
                         reads=[bb["ssq"]], writes=[bb["ssq"]])
                    P.op("dve", lambda e: e.tensor_tensor(out=ksq[:], in0=k_sb[:], in1=k_sb[:], op=ALU.mult), reads=[bb["k_sb"]], writes=[bb["ksq"]])
                    P.op("dve", lambda e: e.tensor_reduce(out=kss[:], in_=ksq[:], axis=AX.X, op=ALU.add), reads=[bb["ksq"]], writes=[bb["kss"]])
                    P.op("dve", lambda e: e.tensor_scalar(out=kss[:], in0=kss[:], scalar1=1.0 / 64, scalar2=EPS, op0=ALU.mult, op1=ALU.add),
                         reads=[bb["kss"]], writes=[bb["kss"]])
                    rope("pool", qi_sb, "qi_sb", 8, i, qi_sb, it2, "qi_sb", "it2", bb)
                    P.op("pool", lambda e: e.tensor_tensor(out=qir[:], in0=qi_sb[:], in1=it2[:], op=ALU.add), reads=[bb["qi_sb"], bb["it2"]], writes=[bb["qir"]])
                    rope("pool", ki_sb, "ki_sb", 1, i, ki_sb, jt2, "ki_sb", "jt2", bb)
                    for dup in range(2):
                        P.op("pool", lambda e, dup=dup: e.tensor_tensor(out=kird[:, dup:dup + 1, dup * 64:(dup + 1) * 64], in0=ki_sb[:], in1=jt2[:], op=ALU.add),
                             reads=[bb["ki_sb"], bb["jt2"]], writes=[bb["kird"]])

                def compute_b(i):
                    rr = i % 2
                    q_sb, qi_sb, k_sb, ki_sb, qr, qir, krd, kird = (q_sb_R[rr], qi_sb_R[rr], k_sb_R[rr], ki_sb_R[rr], qr_R[rr], qir_R[rr], krd_R[rr], kird_R[rr])
                    ssq, kss = ssq_R[rr], kss_R[rr]
                    bb = dict(bb0)
                    bb.update(bbR[rr])
                    tk = slice(i * 128, (i + 1) * 128)
                    P.op("act", lambda e: e.activation(out=ssq[:], in_=ssq[:], func=AF.Sqrt), reads=[bb["ssq"]], writes=[bb["ssq"]])
                    P.op("dve", lambda e: e.reciprocal(out=ssq[:], in_=ssq[:]), reads=[bb["ssq"]], writes=[bb["ssq"]])
                    P.op("dve", lambda e: e.tensor_tensor(out=qg[:], in0=q_sb[:], in1=bc(qnw[:], 8), op=ALU.mult),
                         reads=[bb["q_sb"], b_qnw], writes=[bb["qg"]])
                    rope("dve", qg, "qg", 8, i, qg, sq, "qg", "sq", bb)
                    P.op("dve", lambda e: e.tensor_tensor(out=qg[:], in0=qg[:], in1=sq[:], op=ALU.add), reads=[bb["qg"], bb["sq"]], writes=[bb["qg"]])
                    P.op("dve", lambda e: e.tensor_tensor(out=qr[:], in0=qg[:], in1=ssq[:].unsqueeze(2).to_broadcast([128, 8, 64]), op=ALU.mult),
                         reads=[bb["qg"], bb["ssq"]], writes=[bb["qr"]])
                    P.op("act", lambda e: e.activation(out=kss[:], in_=kss[:], func=AF.Sqrt), reads=[bb["kss"]], writes=[bb["kss"]])
                    P.op("dve", lambda e: e.reciprocal(out=kss[:], in_=kss[:]), reads=[bb["kss"]], writes=[bb["kss"]])
                    P.op("dve", lambda e: e.tensor_tensor(out=kg[:], in0=k_sb[:], in1=bc(knw[:], 2), op=ALU.mult),
                         reads=[bb["k_sb"], b_knw], writes=[bb["kg"]])
                    rope("dve", kg, "kg", 2, i, kt1, kt2, "kt1", "kt2", bb)
                    P.op("dve", lambda e: e.tensor_tensor(out=kt1[:], in0=kt1[:], in1=kt2[:], op=ALU.add), reads=[bb["kt1"], bb["kt2"]], writes=[bb["kt1"]])
                    for dup in range(2):
                        P.op("dve", lambda e, dup=dup: e.tensor_tensor(out=krd[:, :, dup, dup * 64:(dup + 1) * 64], in0=kt1[:], in1=kss[:].unsqueeze(2).to_broadcast([128, 2, 64]),
                                                                        op=ALU.mult), reads=[bb["kt1"], bb["kss"]], writes=[bb["krd"]])

                def transpose_tile(i):
                    rr = i % 2
                    q_sb, qi_sb, k_sb, ki_sb, qr, qir, krd, kird = (q_sb_R[rr], qi_sb_R[rr], k_sb_R[rr], ki_sb_R[rr], qr_R[rr], qir_R[rr], krd_R[rr], kird_R[rr])
                    ssq, kss = ssq_R[rr], kss_R[rr]
                    bb = dict(bb0)
                    bb.update(bbR[rr])
                    tk = slice(i * 128, (i + 1) * 128)
                    pa, pb_ = psTR[0], psTR[1]
                    qrf = qr[:].rearrange("p h d -> p (h d)")
                    qirf = qir[:].rearrange("p h d -> p (h d)")
                    for blk in range(4):
                        P.op("pe", lambda e, blk=blk: e.transpose(out=pa[:, blk, :], in_=qrf[:, blk * 128:(blk + 1) * 128], identity=ident[:]),
                             reads=[bb["qr"], b_ident], writes=[b_psTR[0]], signal=False)
                    for gp in range(4):
                        P.op("pe", lambda e, gp=gp: e.transpose(out=pa[:, 4 + gp, :], in_=krd[:, gp // 2, gp % 2, :], identity=ident[:]),
                             reads=[bb["krd"], b_ident], writes=[b_psTR[0]], signal=(gp == 3))
                    for blk in range(4):
                        P.op("pe", lambda e, blk=blk: e.transpose(out=pb_[:, blk, :], in_=qirf[:, blk * 128:(blk + 1) * 128], identity=ident[:]),
                             reads=[bb["qir"], b_ident], writes=[b_psTR[1]], signal=False)
                    for par in range(2):
                        P.op("pe", lambda e, par=par: e.transpose(out=pb_[:, 4 + par, :], in_=kird[:, par, :], identity=ident[:]),
                             reads=[bb["kird"], b_ident], writes=[b_psTR[1]], signal=(par == 1))
                    P.op("dve", lambda e: e.tensor_copy(out=QT[:, :, tk], in_=pa[:, 0:4, :]), reads=[b_psTR[0]], writes=[b_QT[i]])
                    P.op("dve", lambda e: e.tensor_copy(out=KT[:, :, tk], in_=pa[:, 4:8, :]), reads=[b_psTR[0]], writes=[b_KT[i]])
                    P.op("act", lambda e: e.activation(out=QIT[:, :, tk], in_=pb_[:, 0:4, :], func=AF.Copy), reads=[b_psTR[1]], writes=[b_QIT[i]])
                    P.op("act", lambda e: e.activation(out=KIT[:, :, tk], in_=pb_[:, 4:6, :], func=AF.Copy), reads=[b_psTR[1]], writes=[b_KIT[i]])

                for i_ in range(NT + 2):
                    if 1 <= i_ < NT + 1:
                        compute_b(i_ - 1)
                    if i_ >= 2:
                        transpose_tile(i_ - 2)
                    if i_ < NT:
                        compute_a(i_)

                if stage == 2:
                    dump("QT", QT[:], [128, 4, S], b_QT, BF16)
                    dump("KT", KT[:], [128, 4, S], b_KT, BF16)
                    dump("QIT", QIT[:], [128, 4, S], b_QIT, BF16)
                    dump("KIT", KIT[:], [128, 2, S], b_KIT, BF16)
                    dump("Vx", Vx[:], [128, NT, 2, 65], [b_Vx], BF16)
                    dump("sga", sga[:], [128, NT, 512], b_sga, BF16)
                    dump("wi", wi_all[:], [128, NT, 8], b_wi)
                    P.end()
                    return nc
                P.flush()

            with ExitStack() as L:
                wP = sb("wP", [128, 8, 1024], BF16, L)
                b_wP = P.bufs("wP", 2)
                win_v = win_d.rearrange("(kc p) e -> p kc e", p=128)
                for j in range(2):
                    P.dma("pool", wP[:, :, j * 512:(j + 1) * 512], win_v[:, :, j * 512:(j + 1) * 512], writes=[b_wP[j]])
                wpl = sb("wpl", [128, 4, 128], BF16, L)
                psc = sb("psc", [128, 4], F32, L)
                pinv = sb("pinv", [128, 4, 16], F32, L)
                b_wpl, b_psc, b_pinv = P.buf("wpl"), P.buf("psc"), P.buf("pinv")
                P.dma("pool", wpl[:], wpool_d, writes=[b_wpl])
                P.dma("sp", psc[:], pscale_d, writes=[b_psc])
                P.dma("sp", pinv[:], pinv_d, writes=[b_pinv])
                uT_R = [sb("uT%d" % r_, [128, 16 + S], F32, L) for r_ in range(1)] * 2
                sA = sb("sA", [128, 16 + S], F32, L)
                sB = sb("sB", [128, 16 + S], F32, L)
                sgT_R = [sb("sgT%d" % r_, [128, S], BF16, L) for r_ in range(2)]
                pooled_R = [sb("pooled%d" % r_, [128, S], BF16, L) for r_ in range(2)]
                fix = sb("fix", [128, 16], F32, L)
                b_uT_R, b_sgT_R, b_pooled_R = P.bufs("uT", 1) * 2, P.bufs("sgT", 2), P.bufs("pooled", 2)
                b_sA, b_sB, b_fix = P.buf("sA"), P.buf("sB"), P.buf("fix")
                for t_, b_ in ((uT_R[0], b_uT_R[0]), (sA, b_sA), (sB, b_sB)):
                    P.op("pool", lambda e, t_=t_: e.memset(t_[:, 0:16], 0.0), writes=[b_])
                psU = [ps("psU%d" % i, [128, 512], F32, L) for i in range(2)]
                psM = [ps("psM%d" % i, [128, 512], F32, L) for i in range(2)]
                b_psU, b_psM = P.bufs("psU", 2), P.bufs("psM", 2)
                cnt_u = [0]

                def pool_front(g):
                    rr = g % 2
                    uT, sgT, pooled = uT_R[rr], sgT_R[rr], pooled_R[rr]
                    b_uT, b_sgT, b_pooled = b_uT_R[rr], b_sgT_R[rr], b_pooled_R[rr]
                    for part in range(2):
                        c0 = part * 512 + g * 128
                        for Q in range(4):
                            u_ = cnt_u[0] % 2
                            cnt_u[0] += 1
                            for kc in range(8):
                                P.op("pe", lambda e, kc=kc: e.matmul(out=psU[u_][:, :], lhsT=wP[:, kc, c0:c0 + 128], rhs=hT[:, kc, Q * 512:(Q + 1) * 512],
                                                                     start=(kc == 0), stop=(kc == 7)),
                                     reads=b_hT[4 * Q:4 * Q + 4] + [b_wP[part]], writes=[b_psU[u_]], signal=(kc == 7))
                            if part == 0:
                                P.op("act", lambda e: e.activation(out=uT[:, 16 + Q * 512:16 + (Q + 1) * 512], in_=psU[u_][:, :], func=AF.Copy),
                                     reads=[b_psU[u_]], writes=[b_uT])
                            else:
                                P.op("act", lambda e: e.activation(out=sgT[:, Q * 512:(Q + 1) * 512], in_=psU[u_][:, :], func=AF.Silu),
                                     reads=[b_psU[u_]], writes=[b_sgT])
                    w_ = 2 ** (g + 1)
                    cur, bcur = uT, b_uT
                    ring = [(sA, b_sA), (sB, b_sB)]
                    for lv in range(g + 1):
                        sh = 2 ** lv
                        dst, bdst = ring[lv % 2]
                        P.op("dve", lambda e, cur=cur, dst=dst, sh=sh: e.tensor_tensor(out=dst[:, 16:16 + S], in0=cur[:, 16:16 + S],
                                                                                       in1=cur[:, 16 - sh:16 + S - sh], op=ALU.add),
                             reads=[bcur], writes=[bdst])
                        cur, bcur = dst, bdst
                    P.op("dve", lambda e, cur=cur: e.scalar_tensor_tensor(out=pooled[:], in0=cur[:, 16:16 + S], scalar=1.0 / w_, in1=uT[:, 16:16 + S],
                                                                          op0=ALU.mult, op1=ALU.subtract), reads=[bcur, b_uT], writes=[b_pooled])
                    P.op("dve", lambda e, cur=cur: e.tensor_tensor(out=fix[:], in0=cur[:, 16:32], in1=pinv[:, g, :], op=ALU.mult),
                         reads=[bcur, b_pinv], writes=[b_fix])
                    P.op("dve", lambda e: e.tensor_tensor(out=pooled[:, 0:16], in0=fix[:], in1=uT[:, 16:32], op=ALU.subtract),
                         reads=[b_fix, b_uT], writes=[b_pooled])

                def pool_mix(g):
                    rr = g % 2
                    sgT, pooled = sgT_R[rr], pooled_R[rr]
                    b_sgT, b_pooled = b_sgT_R[rr], b_pooled_R[rr]
                    for Q in range(4):
                        m_ = Q % 2
                        P.op("pe", lambda e: e.matmul(out=psM[m_][:, :], lhsT=wpl[:, g, :], rhs=pooled[:, Q * 512:(Q + 1) * 512], start=True, stop=True),
                             reads=[b_wpl, b_pooled], writes=[b_psM[m_]])
                        P.op("dve", lambda e: e.scalar_tensor_tensor(out=aT[:, g, Q * 512:(Q + 1) * 512], in0=psM[m_][:, :], scalar=psc[:, g:g + 1],
                                                                     in1=sgT[:, Q * 512:(Q + 1) * 512], op0=ALU.mult, op1=ALU.mult),
                             reads=[b_psM[m_], b_psc, b_sgT], writes=[b_aT[g * 4 + Q]])

                for g_ in range(5):
                    if g_ < 4:
                        pool_front(g_)
                    if g_ >= 1:
                        pool_mix(g_ - 1)
                if stage == 3:
                    dump("aT", aT[:, 0:4, :], [128, 4, S], b_aT[0:16], BF16)
                    P.end()
                    return nc
                P.flush()

        with ExitStack() as L:
            NSC = 3
            sc = [sb("sc%d" % i, [128, S], F32, L) for i in range(NSC)]
            Rb = [sb("Rb%d" % i, [128, 512], BF16, L) for i in range(3)]
            Dm = [sb("Dm%d" % i, [128, 8, 128], BF16, L) for i in range(2)]
            maskm = [sb("maskm%d" % i, [128, S], BF16, L) for i in range(2)]
            mb = [sb("mb0", [128, 16, 512], BF16, L), sb("mb1", [128, 12, 512], BF16, L)]
            Pt = [sb("Pt%d" % i, [128, 512], BF16, L) for i in range(3)]
            o_sb = sb("o_sb", [128, 8, 4, 65], F32, L)
            att = sb("att", [128, 4, 8, 64], F32, L)
            attg = sb("attg", [128, 4, 512], BF16, L)
            smt = [sb("sm%d" % i, [128, 8], F32, L) for i in range(2)]
            hkt = [sb("hk%d" % i, [128, KBIS + 1], F32, L) for i in range(2)]
            rden = sb("rden", [128, 8, 4], F32, L)
            psL = [ps("psL%d" % i, [128, 512], F32, L) for i in range(2)]
            psC = [ps("psC%d" % i, [128, 512], F32, L) for i in range(2)]
            psS = [ps("psS%d" % i, [128, 512], F32, L) for i in range(2)]
            psO = ps("psO", [128, 512], F32, L)
            psX = ps("psX", [128, 8, 128], BF16, L)
            b_sc, b_Rb, b_Dm, b_mask, b_mb, b_Pt = P.bufs("sc", NSC), P.bufs("Rb", 3), P.bufs("Dm", 2), P.bufs("maskm", 2), P.bufs("mb", 2), P.bufs("Pt", 3)
            b_osb = P.bufs("osb", 8)
            b_att, b_attg, b_rden = [P.buf(n) for n in "att attg rden".split()]
            b_sm, b_hk = P.bufs("sm", 2), P.bufs("hk", 2)
            b_psL, b_psC, b_psS = P.bufs("psL", 2), P.bufs("psC", 2), P.bufs("psS", 2)
            b_psO, b_psX = P.buf("psO"), P.buf("psX")
            psO3 = psO[:, 0:260].rearrange("p (a b) -> p a b", a=4)
            ctr = {"L": 0, "C": 0, "S": 0}
            order = [0, 1, 2, 3, 4, 5, 6, 7, 12, 13, 14, 15, 8, 9, 10, 11]
            posof = {t_: k_ for k_, t_ in enumerate(order)}
            MBOF = {0: 0, 1: 1, 3: 0, 2: 1}

            def front(i):
                N = 128 * (i + 1)
                d_, scb, b_scb = posof[i] % 2, sc[posof[i] % NSC], b_sc[posof[i] % NSC]
                for h in range(8):
                    P.op("pool", lambda e: e.tensor_scalar(out=Dm[d_][:, h, :], in0=ident[:], scalar1=wi_all[:, i, h:h + 1], scalar2=0.0,
                                                           op0=ALU.mult, op1=ALU.add), reads=[b_ident], writes=[b_Dm[d_]])
                for c in range((N + 511) // 512):
                    w_ = min(512, N - 512 * c)
                    c_ = ctr["C"] % 2
                    ctr["C"] += 1
                    pend = None
                    for h in range(8):
                        par, h4 = h % 2, h // 2
                        l_ = ctr["L"] % 2
                        r_ = ctr["L"] % 3
                        ctr["L"] += 1
                        P.op("pe", lambda e: e.matmul(out=psL[l_][:, 0:w_], lhsT=QIT[:, h4, i * 128:(i + 1) * 128],
                                                      rhs=KIT[:, par, c * 512:c * 512 + w_], start=True, stop=True),
                             writes=[b_psL[l_]])
                        P.op("act", lambda e: e.activation(out=Rb[r_][:, 0:w_], in_=psL[l_][:, 0:w_], func=AF.Relu),
                             reads=[b_psL[l_]], writes=[b_Rb[r_]])
                        if pend is not None:
                            ph, pr = pend
                            P.op("pe", lambda e: e.matmul(out=psC[c_][:, 0:w_], lhsT=Dm[d_][:, ph, :], rhs=Rb[pr][:, 0:w_], start=(ph == 0), stop=False),
                                 reads=[b_Dm[d_], b_Rb[pr]], writes=[b_psC[c_]], signal=True)
                        pend = (h, r_)
                    ph, pr = pend
                    P.op("pe", lambda e: e.matmul(out=psC[c_][:, 0:w_], lhsT=Dm[d_][:, ph, :], rhs=Rb[pr][:, 0:w_], start=False, stop=True),
                         reads=[b_Dm[d_], b_Rb[pr]], writes=[b_psC[c_]], signal=True)
                    P.op("act", lambda e: e.activation(out=scb[:, c * 512:c * 512 + w_], in_=psC[c_][:, 0:w_], func=AF.Copy),
                         reads=[b_psC[c_]], writes=[b_scb])

            def bisect(i):
                N = 128 * (i + 1)
                scb, b_scb = sc[posof[i] % NSC], b_sc[posof[i] % NSC]
                sm, bsm, hk, bhk = smt[posof[i] % 2], b_sm[posof[i] % 2], hkt[posof[i] % 2], b_hk[posof[i] % 2]
                RMAX, RMIN, R0, TT, CNT, DD, THR = [sm[:, k:k + 1] for k in range(7)]
                if i >= 2:
                    P.op("dve", lambda e: e.tensor_reduce(out=RMAX, in_=scb[:, 0:N], axis=AX.X, op=ALU.max, apply_absolute_value=True),
                         reads=[b_scb], writes=[bsm])
                P.op("dve", lambda e: e.tensor_tensor(out=scb[:, N - 128:N], in0=scb[:, N - 128:N], in1=causal[:], op=ALU.add),
                     reads=[b_scb, b_causal], writes=[b_scb])
                if i >= 2:
                    P.op("dve", lambda e: e.tensor_scalar(out=hk[:], in0=pow2[:], scalar1=RMAX, scalar2=2.0, op0=ALU.mult, op1=ALU.mult),
                         reads=[bsm, b_pow2], writes=[bhk])
                    P.op("dve", lambda e: e.tensor_tensor(out=TT, in0=hk[:, 0:1], in1=RMAX, op=ALU.subtract), reads=[bhk, bsm], writes=[bsm])
                    for k in range(KBIS):
                        P.op("dve", lambda e: e.tensor_scalar(out=maskm[posof[i] % 2][:, 0:N], in0=scb[:, 0:N], scalar1=TT, scalar2=None, op0=ALU.is_ge,
                                                              op1=ALU.add, accum_out=CNT), reads=[b_scb, bsm], writes=[b_mask[posof[i] % 2], bsm])
                        P.op("dve", lambda e: e.tensor_scalar(out=DD, in0=CNT, scalar1=255.5, scalar2=0.5, op0=ALU.is_ge, op1=ALU.subtract),
                             reads=[bsm], writes=[bsm])
                        P.op("dve", lambda e: e.scalar_tensor_tensor(out=TT, in0=DD, scalar=hk[:, k:k + 1], in1=TT, op0=ALU.mult, op1=ALU.add),
                             reads=[bsm, bhk], writes=[bsm])
                    P.op("dve", lambda e: e.tensor_tensor(out=THR, in0=TT, in1=hk[:, KBIS:KBIS + 1], op=ALU.subtract), reads=[bsm, bhk], writes=[bsm])
                    thr_ap, thr_b = THR, bsm
                else:
                    thr_ap, thr_b = thr0[:, 0:1], b_thr0
                mk, b_mk = maskm[posof[i] % 2], b_mask[posof[i] % 2]
                P.op("dve", lambda e: e.tensor_scalar(out=mk[:, 0:N], in0=scb[:, 0:N], scalar1=thr_ap, scalar2=1.0, op0=ALU.is_ge, op1=ALU.subtract),
                     reads=[b_scb, thr_b], writes=[b_mk])

            def trans(i):
                qc, il = i // 4, i % 4
                mbq, b_mbq = mb[MBOF[qc]], b_mb[MBOF[qc]]
                mk, b_mk = maskm[posof[i] % 2], b_mask[posof[i] % 2]
                for j0 in range(0, i + 1, 8):
                    n_ = min(8, i + 1 - j0)
                    for jj in range(n_):
                        j = j0 + jj
                        P.op("pe", lambda e: e.transpose(out=psX[:, jj, :], in_=mk[:, j * 128:(j + 1) * 128], identity=ident[:]),
                             reads=[b_mk, b_ident], writes=[b_psX], signal=(jj == n_ - 1))
                    P.op("act", lambda e: e.activation(out=mbq[:, j0:j0 + n_, il * 128:(il + 1) * 128], in_=psX[:, 0:n_, :], func=AF.Copy, scale=NEGM),
                         reads=[b_psX], writes=[b_mbq])

            def attn_head(qc, h):
                mbq, b_mbq = mb[MBOF[qc]], b_mb[MBOF[qc]]
                nj = 4 * (qc + 1)
                par, h4, g = h % 2, h // 2, h // 4
                state = {"first": True}

                def pv(j, p_):
                    ils = [il for il in range(4) if j <= 4 * qc + il]
                    for il in ils:
                        last = (j == nj - 1) and (il == ils[-1])
                        P.op("pe", lambda e: e.matmul(out=psO3[:, il, :], lhsT=Pt[p_][:, il * 128:(il + 1) * 128], rhs=Vx[:, j, g, :],
                                                      start=state["first"], stop=last, skip_group_check=True),
                             reads=[b_Pt[p_], b_Vx], writes=[b_psO], signal=(il == ils[-1]))
                        state["first"] = False

                prev = None
                for j in range(nj):
                    s_ = ctr["S"] % 2
                    p_ = ctr["S"] % 3
                    ctr["S"] += 1
                    t0 = 128 * max(0, j - 4 * qc)
                    P.op("pe", lambda e: e.matmul(out=psS[s_][:, t0:512], lhsT=KT[:, g * 2 + par, j * 128:(j + 1) * 128],
                                                  rhs=QT[:, h4, qc * 512 + t0:(qc + 1) * 512], start=True, stop=False),
                         writes=[b_psS[s_]], signal=False)
                    P.op("pe", lambda e: e.matmul(out=psS[s_][:, t0:512], lhsT=ident[:], rhs=mbq[:, j, t0:512], start=False, stop=True),
                         reads=[b_mbq, b_ident], writes=[b_psS[s_]])
                    P.op("act", lambda e: e.activation(out=Pt[p_][:, t0:512], in_=psS[s_][:, t0:512], func=AF.Exp, scale=0.125),
                         reads=[b_psS[s_]], writes=[b_Pt[p_]])
                    if prev is not None:
                        pv(*prev)
                    prev = (j, p_)
                pv(*prev)
                P.op("act", lambda e: e.activation(out=o_sb[:, h, :, :], in_=psO3, func=AF.Copy), reads=[b_psO], writes=[b_osb[h]])

            def attn_final(qc):
                P.op("dve", lambda e: e.reciprocal(out=rden[:], in_=o_sb[:, :, :, 64]), reads=b_osb, writes=[b_rden])
                P.op("dve", lambda e: e.tensor_tensor(out=att[:].rearrange("p a h d -> p h a d"), in0=o_sb[:, :, :, 0:64],
                                                      in1=rden[:].unsqueeze(3).to_broadcast([128, 8, 4, 64]), op=ALU.mult),
                     reads=b_osb + [b_rden], writes=[b_att])
                P.op("pool", lambda e: e.tensor_tensor(out=attg[:], in0=att[:].rearrange("p a h d -> p a (h d)"), in1=sga[:, 4 * qc:4 * qc + 4, :], op=ALU.mult),
                     reads=[b_att], writes=[b_attg])
                for half in range(2):
                    for bl in range(2):
                        blk = half * 2 + bl
                        for il in range(4):
                            P.op("pe", lambda e: e.transpose(out=psX[:, bl * 4 + il, :], in_=attg[:, il, blk * 128:(blk + 1) * 128], identity=ident[:]),
                                 reads=[b_attg, b_ident], writes=[b_psX], signal=(bl == 1 and il == 3))
                    for bl in range(2):
                        blk = half * 2 + bl
                        P.op("act", lambda e: e.activation(out=aT[:, 4 + blk, qc * 512:(qc + 1) * 512],
                                                           in_=psX[:, bl * 4:bl * 4 + 4, :].rearrange("p a b -> p (a b)"), func=AF.Copy),
                             reads=[b_psX], writes=[b_aT[16 + blk * 4 + qc]])

            queue = []
            mb_owner = {0: None, 1: None}

            def run_unit(u):
                kind, qc_, h_ = u
                attn_head(qc_, h_) if kind == "h" else attn_final(qc_)

            for s_ in range(NT + 3):
                if s_ < NT:
                    front(order[s_])
                if 2 <= s_ < NT + 2:
                    bisect(order[s_ - 2])
                if 3 <= s_:
                    t_ = order[s_ - 3]
                    qc_t = t_ // 4
                    if t_ % 4 == 0:
                        prev_owner = mb_owner[MBOF[qc_t]]
                        if prev_owner is not None:
                            while any(u[1] == prev_owner for u in queue):
                                run_unit(queue.pop(0))
                        mb_owner[MBOF[qc_t]] = qc_t
                    trans(t_)
                    if t_ % 4 == 3:
                        queue += [("h", qc_t, h) for h in range(8)] + [("f", qc_t, 0)]
                for _ in range(2):
                    if queue:
                        run_unit(queue.pop(0))
            while queue:
                run_unit(queue.pop(0))
            if stage == 4:
                dump("aT", aT[:], [128, 8, S], b_aT, BF16)
                P.end()
                return nc
            P.flush()

        with ExitStack() as L:
            wst = [sb("wst%d" % i, [128, 1024], F32, L) for i in range(4)]
            wo = sb("wo", [128, 8, 1024], BF16, L)
            xr = [sb("xr%d" % i, [128, 1024], F32, L) for i in range(4)]
            ot = [sb("ot%d" % i, [128, 1024], F32, L) for i in range(3)]
            psY = [ps("psY%d" % i, [128, 512], F32, L) for i in range(4)]
            b_wst, b_xr, b_ot, b_psY = P.bufs("wst", 4), P.bufs("xr", 4), P.bufs("ot", 3), P.bufs("psY", 4)
            b_wo = P.bufs("wo", 8)
            wout_v = wout_d.rearrange("(ec p) d -> ec p d", p=128)
            for ec in range(8):
                k = ec % 4
                P.dma("sp", wst[k][:], wout_v[ec], writes=[b_wst[k]])
                P.op("pool" if ec % 2 else "dve", lambda e: e.tensor_tensor(out=wo[:, ec, :], in0=wst[k][:], in1=gate_b[:], op=ALU.mult),
                     reads=[b_wst[k], b_gate], writes=[b_wo[ec]])
            xv = x_d.rearrange("(n p) d -> n p d", p=128)
            ov = out_d.rearrange("(n p) d -> n p d", p=128)
            for i in range(NT):
                kx = i % 4
                ko = i % 3
                P.dma("sp", xr[kx][:], xv[i], writes=[b_xr[kx]])
                for hf in range(2):
                    y_ = (i % 2) * 2 + hf
                    for ec in range(8):
                        P.op("pe", lambda e: e.matmul(out=psY[y_][:, :], lhsT=aT[:, ec, i * 128:(i + 1) * 128], rhs=wo[:, ec, hf * 512:(hf + 1) * 512],
                                                      start=(ec == 0), stop=(ec == 7)), reads=[b_wo[ec]], writes=[b_psY[y_]], signal=(ec == 7))
                    P.op("dve", lambda e: e.tensor_tensor(out=ot[ko][:, hf * 512:(hf + 1) * 512], in0=psY[y_][:, :], in1=xr[kx][:, hf * 512:(hf + 1) * 512], op=ALU.add),
                         reads=[b_psY[y_], b_xr[kx]], writes=[b_ot[ko]])
                P.dma("act", ov[i], ot[ko][:], reads=[b_ot[ko]], out_final=True)
            P.end()
    return nc


def make_in_maps(x, c, norm_w, w_ada, b_ada, w_in, q_norm_w, k_norm_w, w_pool, pool_scale, w_out, cores=range(8)):
    f = lambda a: np.ascontiguousarray(np.asarray(a, dtype=np.float32))
    hc = host_consts()
    shared = {
        "nw_col": f(np.asarray(norm_w)[0].reshape(8, 128).T),
        "bada_col": f(np.asarray(b_ada)[0, :2048].reshape(16, 128).T),
        "bgate_row": f(np.asarray(b_ada)[0, 2048:].reshape(1, 1024)),
        "w_ada": f(np.asarray(w_ada)[0]),
        "w_in": f(np.asarray(w_in)[0]),
        "w_out": f(np.asarray(w_out)[0]),
        "w_pool": f(np.asarray(w_pool)[0].transpose(1, 0, 2)),
        "pscale_col": f(np.asarray(pool_scale)[0].reshape(4, 128).T),
        "qnw_b": f(np.tile(np.asarray(q_norm_w)[0][None, :], (128, 1))),
        "knw_b": f(np.tile(np.asarray(k_norm_w)[0][None, :], (128, 1))),
    }
    shared.update(hc)
    maps = []
    for b in cores:
        m = dict(shared)
        m["x"] = f(np.asarray(x)[b])
        m["c_col"] = f(np.asarray(c)[b].reshape(8, 128).T)
        maps.append(m)
    return maps


_NC_CACHE = {}


def kernel(x, c, norm_w, w_ada, b_ada, w_in, q_norm_w, k_norm_w, w_pool, pool_scale, w_out):
    if "nc" not in _NC_CACHE:
        _NC_CACHE["nc"] = build()
    nc = _NC_CACHE["nc"]
    maps = make_in_maps(x, c, norm_w, w_ada, b_ada, w_in, q_norm_w, k_norm_w, w_pool, pool_scale, w_out)
    res = run_bass_kernel_spmd(nc, maps, core_ids=list(range(8)))
    return np.stack([np.asarray(r["out"], dtype=np.float32) for r in res.results], axis=0)
```
